# Optimizing a Trainium2 kernel written in Bass

```python
import math
import jax, jax.numpy as jnp
from jax import lax
import numpy as np


D_MODEL = 2048
BATCH = 4
SEQ = 4096
DEPTH = 4

GRID_W = 64
CTX_LEN = 256
BRANCH_WIDTH = D_MODEL // 2
N_BRANCH = 3
A_DH = 64
A_DV = 2 * A_DH
A_HEADS = BRANCH_WIDTH // A_DV
A_QK_WIDTH = A_HEADS * 2 * A_DH
A_SCALE = A_DH ** -0.5
A_Q_BLOCK = 128
ROPE_THETA = 10000.0
ROPE_PAIRS = A_DH // 4
LAMBDA_STD = 0.1
B_CHUNK = 128
B_GROUPS = 8
B_GDIM = BRANCH_WIDTH // B_GROUPS
C_WINDOWS = (2, 4, 8, 16)
C_GROUPS = len(C_WINDOWS)
C_GDIM = BRANCH_WIDTH // C_GROUPS
FFN_HIDDEN = ((8 * D_MODEL + 3 * 256 - 1) // (3 * 256)) * 256
ALPHA = (2 * DEPTH) ** 0.25
BETA = (8 * DEPTH) ** -0.25
LN_EPS = 1e-6
Q_OFF = 0
K_OFF = Q_OFF + A_QK_WIDTH
V_OFF = K_OFF + A_QK_WIDTH
V_END = V_OFF + BRANCH_WIDTH
BU_OFF = V_END
C_OFF = BU_OFF + 2 * BRANCH_WIDTH
G_OFF = C_OFF + BRANCH_WIDTH
IN_WIDTH = G_OFF + N_BRANCH * D_MODEL

kernel_name = 'hybrid_diffattn_gmlp_pool_dit_block'

f32 = jnp.float32


def norm_only(x):
    xf = x.astype(f32)
    mu = jnp.mean(xf, -1, keepdims=True)
    var = jnp.mean(jnp.square(xf - mu), -1, keepdims=True)
    return ((xf - mu) * lax.rsqrt(var + LN_EPS)).astype(x.dtype)


def layer_norm(x, g, b):
    xf = x.astype(f32)
    mu = jnp.mean(xf, -1, keepdims=True)
    var = jnp.mean(jnp.square(xf - mu), -1, keepdims=True)
    return ((xf - mu) * lax.rsqrt(var + LN_EPS) * g + b).astype(x.dtype)


def rms_norm(x, g):
    xf = x.astype(f32)
    return (xf * lax.rsqrt(jnp.mean(jnp.square(xf), -1, keepdims=True) + LN_EPS) * g).astype(x.dtype)


def adaln(cond, w, b, n_chunks):
    width = n_chunks * D_MODEL
    return jax.nn.silu(cond) @ w[:, :width] + b[:width]


def axial_rope_tables(row, col):
    inv = ROPE_THETA ** (-jnp.arange(ROPE_PAIRS, dtype=f32) / ROPE_PAIRS)
    ang = jnp.stack([row.astype(f32)[:, None] * inv, col.astype(f32)[:, None] * inv], axis=1)
    return jnp.cos(ang), jnp.sin(ang)


def apply_rope(x, cos, sin):
    xr = x.reshape(x.shape[:-1] + (2, 2, ROPE_PAIRS)).astype(f32)
    x1, x2 = xr[..., 0, :], xr[..., 1, :]
    cb, sb = cos[:, None, None], sin[:, None, None]
    out = jnp.stack([x1 * cb - x2 * sb, x1 * sb + x2 * cb], axis=-2)
    return out.reshape(x.shape).astype(x.dtype)


def heads_qk(z):
    return z.reshape(z.shape[0], z.shape[1], A_HEADS, 2, A_DH)


def heads_v(z):
    return z.reshape(z.shape[0], z.shape[1], A_HEADS, A_DV)


def diff_attention(q, k, v, lam):
    s = jnp.einsum('bqhmd,bkhmd->bhmqk', q, k).astype(f32) * A_SCALE
    p = jax.nn.softmax(s, axis=-1)
    a = (p[:, :, 0] - lam * p[:, :, 1]).astype(v.dtype)
    return jnp.einsum('bhqk,bkhd->bqhd', a, v)


def latent_diff_attention(q, k, v, lam):
    bsz, n, h, m, d = q.shape
    nblk = n // A_Q_BLOCK
    qb = jnp.moveaxis(q.reshape(bsz, nblk, A_Q_BLOCK, h, m, d), 1, 0)
    ob = lax.map(lambda blk: diff_attention(blk, k, v, lam), qb)
    return jnp.moveaxis(ob, 0, 1).reshape(bsz, n, h, v.shape[-1])


def diff_post(o, g, lam_init):
    o = rms_norm(o, g) * (1.0 - lam_init)
    return o.reshape(o.shape[0], o.shape[1], BRANCH_WIDTH)


def gmlp_branch(z_uv, ln_g, ln_b, w_s, b_s):
    z = jax.nn.gelu(z_uv, approximate=False)
    u, v = z[..., :BRANCH_WIDTH], z[..., BRANCH_WIDTH:]
    v = layer_norm(v, ln_g, ln_b)
    bsz, n, _ = v.shape
    v = v.reshape(bsz, n // B_CHUNK, B_CHUNK, B_GROUPS, B_GDIM)
    s = jnp.einsum('gij,bnjgc->bnigc', w_s, v) + b_s.T[:, :, None]
    return u * s.reshape(bsz, n, BRANCH_WIDTH)


def pool_branch(z, w_pool, scale):
    bsz, n, _ = z.shape
    zf = z.reshape(bsz, n, C_GROUPS, C_GDIM).astype(f32)
    cs = jnp.concatenate([jnp.zeros_like(zf[:, :1]), jnp.cumsum(zf, axis=1)], axis=1)
    t = jnp.arange(n)
    pooled = []
    for g, w in enumerate(C_WINDOWS):
        lo = jnp.clip(t - w // 2, 0, n)
        hi = jnp.clip(t - w // 2 + w, 0, n)
        cs_g = cs[:, :, g]
        pooled.append((cs_g[:, hi] - cs_g[:, lo]) / (hi - lo).astype(f32)[None, :, None])
    d = (jnp.stack(pooled, axis=2) - zf).astype(z.dtype)
    y = jnp.einsum('blgc,gcd->blgd', d, w_pool)
    return y.reshape(bsz, n, BRANCH_WIDTH) * scale


def merge_branches(z_gate, y_a, y_b, y_c, w_branch, w_out):
    bsz, n, _ = z_gate.shape
    ys = jnp.stack([y_a, y_b, y_c], axis=2)
    proj = jnp.einsum('blnw,nwd->blnd', ys, w_branch)
    gates = jax.nn.sigmoid(z_gate.reshape(bsz, n, N_BRANCH, D_MODEL))
    return jnp.sum(gates * proj, axis=2) @ w_out


def swiglu(h, w_gu, w_down):
    z = h @ w_gu
    return (jax.nn.silu(z[..., :FFN_HIDDEN]) * z[..., FFN_HIDDEN:]) @ w_down


def setup_inputs(seed: int = 0) -> dict:
    key = jax.random.key(seed)
    ks = jax.random.split(key, 24)

    def nrm(k, shape, scale):
        return jax.random.normal(k, shape, f32) * scale

    return {
        'x': nrm(ks[0], (BATCH, SEQ, D_MODEL), 1.0),
        'c': nrm(ks[1], (BATCH, D_MODEL), 1.0),
        'ctx': nrm(ks[2], (BATCH, CTX_LEN, D_MODEL), 1.0),
        'c_ctx': nrm(ks[3], (D_MODEL,), 1.0),
        'w_ada': nrm(ks[4], (DEPTH, D_MODEL, 6 * D_MODEL), 0.5 * D_MODEL ** -0.5),
        'b_ada': nrm(ks[5], (DEPTH, 6 * D_MODEL), 0.02),
        'w_in': nrm(ks[6], (DEPTH, D_MODEL, IN_WIDTH), D_MODEL ** -0.5),
        'lam_qk': nrm(ks[7], (DEPTH, 4, A_DH), LAMBDA_STD),
        'subln_g': 1.0 + nrm(ks[8], (DEPTH, A_DV), 0.1),
        'gmlp_ln_g': 1.0 + nrm(ks[9], (DEPTH, BRANCH_WIDTH), 0.1),
        'gmlp_ln_b': nrm(ks[10], (DEPTH, BRANCH_WIDTH), 0.02),
        'w_spatial': nrm(ks[11], (DEPTH, B_GROUPS, B_CHUNK, B_CHUNK), B_CHUNK ** -0.5),
        'b_spatial': 1.0 + nrm(ks[12], (DEPTH, B_GROUPS, B_CHUNK), 0.1),
        'w_pool': nrm(ks[13], (DEPTH, C_GROUPS, C_GDIM, C_GDIM), C_GDIM ** -0.5),
        'pool_scale': 1.0 + nrm(ks[14], (DEPTH, BRANCH_WIDTH), 0.1),
        'w_branch': nrm(ks[15], (DEPTH, N_BRANCH, BRANCH_WIDTH, D_MODEL), BRANCH_WIDTH ** -0.5),
        'w_out': nrm(ks[16], (DEPTH, D_MODEL, D_MODEL), BETA * D_MODEL ** -0.5),
        'ln1_g': 1.0 + nrm(ks[17], (DEPTH, D_MODEL), 0.1),
        'ln1_b': nrm(ks[18], (DEPTH, D_MODEL), 0.02),
        'w_gu': nrm(ks[19], (DEPTH, D_MODEL, 2 * FFN_HIDDEN), D_MODEL ** -0.5),
        'w_down': nrm(ks[20], (DEPTH, FFN_HIDDEN, D_MODEL), BETA * FFN_HIDDEN ** -0.5),
        'ln2_g': 1.0 + nrm(ks[21], (DEPTH, D_MODEL), 0.1),
        'ln2_b': nrm(ks[22], (DEPTH, D_MODEL), 0.02),
    }


def reference(x, c, ctx, c_ctx, w_ada, b_ada, w_in, lam_qk, subln_g, gmlp_ln_g, gmlp_ln_b,
              w_spatial, b_spatial, w_pool, pool_scale, w_branch, w_out, ln1_g, ln1_b,
              w_gu, w_down, ln2_g, ln2_b):
    n_lat = x.shape[1]
    rows = n_lat // GRID_W
    row = jnp.repeat(jnp.arange(rows), GRID_W)
    col = jnp.tile(jnp.arange(GRID_W), rows)
    cos, sin = axial_rope_tables(row, col)

    x = norm_only(x)
    ctx = norm_only(ctx)

    for l in range(DEPTH):
        last = l == DEPTH - 1
        lam_init = 0.8 - 0.6 * math.exp(-0.3 * l)
        lq = lam_qk[l].astype(f32)
        lam = jnp.exp(jnp.sum(lq[0] * lq[1])) - jnp.exp(jnp.sum(lq[2] * lq[3])) + lam_init

        sh_m, sc_m, g_m, sh_f, sc_f, g_f = jnp.split(adaln(c, w_ada[l], b_ada[l], 6)[:, None, :], 6, axis=-1)
        n_ctx_mod = 2 if last else 6
        mods_c = jnp.split(adaln(c_ctx, w_ada[l], b_ada[l], n_ctx_mod), n_ctx_mod, axis=-1)

        h = x * (1.0 + sc_m) + sh_m
        hc = ctx * (1.0 + mods_c[1]) + mods_c[0]
        z = h @ w_in[l]
        if last:
            zc_kv = hc @ w_in[l][:, K_OFF:V_END]
        else:
            zc = hc @ w_in[l]
            zc_kv = zc[..., K_OFF:V_END]
        k_c = heads_qk(zc_kv[..., :A_QK_WIDTH])
        v_c = heads_v(zc_kv[..., A_QK_WIDTH:])

        q = apply_rope(heads_qk(z[..., Q_OFF:K_OFF]), cos, sin)
        k = apply_rope(heads_qk(z[..., K_OFF:V_OFF]), cos, sin)
        v = heads_v(z[..., V_OFF:V_END])
        k_all = jnp.concatenate([k_c, k], axis=1)
        v_all = jnp.concatenate([v_c, v], axis=1)
        y_a = diff_post(latent_diff_attention(q, k_all, v_all, lam), subln_g[l], lam_init)
        y_b = gmlp_branch(z[..., BU_OFF:C_OFF], gmlp_ln_g[l], gmlp_ln_b[l], w_spatial[l], b_spatial[l])
        y_c = pool_branch(z[..., C_OFF:G_OFF], w_pool[l], pool_scale[l])
        out = merge_branches(z[..., G_OFF:], y_a, y_b, y_c, w_branch[l], w_out[l])
        x = layer_norm(ALPHA * x + g_m * out, ln1_g[l], ln1_b[l])

        hf = x * (1.0 + sc_f) + sh_f
        x = layer_norm(ALPHA * x + g_f * swiglu(hf, w_gu[l], w_down[l]), ln2_g[l], ln2_b[l])

        if not last:
            _, _, g_mc, sh_fc, sc_fc, g_fc = mods_c
            q_c = heads_qk(zc[..., Q_OFF:K_OFF])
            y_ac = diff_post(diff_attention(q_c, k_c, v_c, lam), subln_g[l], lam_init)
            y_bc = gmlp_branch(zc[..., BU_OFF:C_OFF], gmlp_ln_g[l], gmlp_ln_b[l], w_spatial[l], b_spatial[l])
            y_cc = pool_branch(zc[..., C_OFF:G_OFF], w_pool[l], pool_scale[l])
            out_c = merge_branches(zc[..., G_OFF:], y_ac, y_bc, y_cc, w_branch[l], w_out[l])
            ctx = layer_norm(ALPHA * ctx + g_mc * out_c, ln1_g[l], ln1_b[l])
            hfc = ctx * (1.0 + sc_fc) + sh_fc
            ctx = layer_norm(ALPHA * ctx + g_fc * swiglu(hfc, w_gu[l], w_down[l]), ln2_g[l], ln2_b[l])

    return x
```

```python
import math
import numpy as np
import ml_dtypes
from contextlib import ExitStack
import concourse.bass as bass
import concourse.mybir as mybir
from concourse.bass_utils import run_bass_kernel_spmd

F32 = mybir.dt.float32
BF16 = mybir.dt.bfloat16
F32R = mybir.dt.float32r
AF = mybir.ActivationFunctionType
ALU = mybir.AluOpType

ENGS = ("pe", "act", "dve", "pool", "sp")
N_DMA_SEMS = 60

D = 2048
KC = 16
DEPTH = 4
FFN = 5632
FKC = 44
IN_W = 12288
K_OFF, V_OFF, BU_OFF, BV_OFF, C_OFF, G_OFF = 1024, 2048, 3072, 4096, 5120, 6144
ALPHA = (2 * DEPTH) ** 0.25
EPS = 1e-6
A_SCALE = 0.125
WINS = (2, 4, 8, 16)


class Buf:
    def __init__(self, t=None, dsem=None, name=""):
        self.t = t
        self.dsem = dsem
        self.w = None
        self.r = []
        self.name = name

    def __getitem__(self, k):
        return self.t[k]


class Rot:
    def __init__(self, bufs):
        self.bufs = bufs
        self.i = 0

    def next(self):
        b = self.bufs[self.i % len(self.bufs)]
        self.i += 1
        return b


class KB:
    def __init__(self, nc, stack):
        self.nc = nc
        self.sem = {}
        self.cnt = {}
        for e in ENGS:
            self.sem[e] = stack.enter_context(nc.semaphore("s_" + e))
            self.cnt[e] = 0
        self.dma_sems = []
        for i in range(N_DMA_SEMS):
            n = "d%d" % i
            self.sem[n] = stack.enter_context(nc.semaphore("s_" + n))
            self.cnt[n] = 0
            self.dma_sems.append(n)
        self.seen = {e: {} for e in ENGS}
        self.ops = {e: [] for e in ENGS}
        self.next_dsem = 0
        self.stage_bufs = []
        self.stage_dsems = set()
        self.nstage = 0

    def buf(self, t=None, dma=True, name=""):
        ds = None
        if dma:
            ds = self.dma_sems[self.next_dsem % N_DMA_SEMS]
            self.next_dsem += 1
        b = Buf(t, ds, name)
        self.stage_bufs.append(b)
        return b

    def _waits(self, e, reads, writes):
        need = {}

        def add(ev):
            if ev is None:
                return
            s, c = ev
            if e == "pe" and s == "pe":
                return
            if self.seen[e].get(s, 0) < c:
                need[s] = max(need.get(s, 0), c)

        for b in reads:
            add(b.w)
        for b in writes:
            add(b.w)
            for ev in b.r:
                add(ev)
        for s, c in need.items():
            self.seen[e][s] = c
            h = self.sem[s]
            self.ops[e].append(lambda eng, h=h, c=c: eng.wait_ge(h, c))

    def _mark(self, ev, reads, writes):
        for b in reads:
            if b not in writes:
                b.r.append(ev)
        for b in writes:
            b.w = ev
            b.r = []

    def op(self, e, fn, reads=(), writes=(), inc=True):
        self._waits(e, reads, writes)
        h = self.sem[e]
        if inc:
            self.cnt[e] += 1
            ev = (e, self.cnt[e])
            self.ops[e].append(lambda eng, fn=fn, h=h: fn(eng).then_inc(h, 1))
        else:
            ev = (e, self.cnt[e] + 1)
            self.ops[e].append(lambda eng, fn=fn: fn(eng))
        self._mark(ev, reads, writes)

    def dma(self, e, out, in_, reads=(), writes=(), dsem=None, **kw):
        if dsem is None:
            for b in list(writes) + list(reads):
                if b.dsem is not None:
                    dsem = b.dsem
                    break
        assert dsem is not None
        self._waits(e, reads, writes)
        self.cnt[dsem] += 16
        ev = (dsem, self.cnt[dsem])
        h = self.sem[dsem]
        self.stage_dsems.add(dsem)
        self.ops[e].append(lambda eng, out=out, in_=in_, h=h, kw=kw:
                           eng.dma_start(out=out, in_=in_, **kw).then_inc(h, 16))
        self._mark(ev, reads, writes)

    def mm_group(self, ps, ps_ap, lhs_list, rhs_list, reads, reads_k=None):
        n = len(lhs_list)
        for k in range(n):
            rd = list(reads) + ([reads_k[k]] if reads_k is not None else [])
            self.op("pe", lambda e, k=k: e.matmul(ps_ap, lhsT=lhs_list[k], rhs=rhs_list[k],
                                                  start=(k == 0), stop=(k == n - 1)),
                    reads=rd, writes=[ps], inc=(k == n - 1))

    def run_stage(self, name=None):
        nc = self.nc
        self.nstage += 1
        name = "%s_%d" % (name or "st", self.nstage)
        for s in sorted(self.stage_dsems):
            c = self.cnt[s]
            if self.seen["sp"].get(s, 0) < c:
                h = self.sem[s]
                self.ops["sp"].append(lambda eng, h=h, c=c: eng.wait_ge(h, c))
        for e in ("pe", "act", "dve", "pool"):
            c = self.cnt[e]
            if c > 0 and self.seen["sp"].get(e, 0) < c:
                h = self.sem[e]
                self.ops["sp"].append(lambda eng, h=h, c=c: eng.wait_ge(h, c))
        ops = self.ops
        with nc.Block(name) as block:
            @block.sync
            def _(eng):
                for f in ops["sp"]:
                    f(eng)

            @block.tensor
            def _(eng):
                for f in ops["pe"]:
                    f(eng)

            @block.scalar
            def _(eng):
                for f in ops["act"]:
                    f(eng)

            @block.vector
            def _(eng):
                for f in ops["dve"]:
                    f(eng)

            @block.gpsimd
            def _(eng):
                for f in ops["pool"]:
                    f(eng)
        for e in ENGS:
            for s in self.cnt:
                self.seen[e][s] = self.cnt[s]
        self.ops = {e: [] for e in ENGS}
        for b in self.stage_bufs:
            b.w = None
            b.r = []
        self.stage_bufs = []
        self.stage_dsems = set()


class Cfg:
    def __init__(self, lat, depth=DEPTH):
        self.LAT = lat
        self.CTX = 128
        self.T = lat + 128
        self.NT = self.T // 128
        self.NTL = lat // 128
        self.depth = depth
        self.blocks = []
        t = 0
        while t < lat:
            bs = min(512, lat - t)
            self.blocks.append((t, bs))
            t += bs
        self.lat_blocks = list(self.blocks)
        self.blocks.append((lat, 128))


class Prog:
    def __init__(self, cfg, phases):
        self.cfg = cfg
        self.nc = bass.Bass("TRN2", target_bir_lowering=False)
        self.phases = phases
        self.inputs = []
        self.outputs = []
        self.dr = {}
        self.fused = False
        self.wdep = 1
        self.drt = {}
        self.ccs = None
        self.ccnt = 0

    def li(self, l):
        return l if self.fused else 0

    def din(self, name, shape, dt=F32):
        if name not in self.dr:
            self.dr[name] = self.nc.dram_tensor(name, list(shape), dt, kind="ExternalInput").ap()
            self.inputs.append(name)
        return self.dr[name]

    def dout(self, name, shape, dt=F32):
        if name not in self.dr:
            self.dr[name] = self.nc.dram_tensor(name, list(shape), dt, kind="ExternalOutput").ap()
            self.outputs.append(name)
        return self.dr[name]

    def dcc(self, name, shape, dt=F32):
        if name not in self.dr:
            t = self.nc.dram_tensor(name, list(shape), dt)
            self.drt[name] = t
            self.dr[name] = t.ap()
        return self.dr[name]

    def ph_xchg(self, l):
        kb = self.kb
        cfg = self.cfg
        T = cfg.T
        for h in range(8):
            self.dcc("KT_x%d" % h, (128, T), BF16)
            self.dcc("V_x%d" % h, (T, 128), BF16)
        self.dcc("halo_x", (8 * 128, 32), F32)
        self._gathered()
        if self.ccs is None:
            self.ccs = self._stack.enter_context(self.nc.semaphore("cc_sem"))
        ccs = self.ccs
        pairs = [[0, 1], [2, 3], [4, 5], [6, 7]]
        names = [("KT_x%d" % h, "KT_all%d" % h) for h in range(8)] + [("V_x%d" % h, "V_all%d" % h) for h in range(8)] + [("halo_x", "halo_all")]
        for a, b in names:
            ta, tb = self.drt[a], self.drt[b]
            kb.ops["pool"].append(lambda eng, ta=ta, tb=tb: eng.collective_compute(
                "AllGather", ALU.bypass, replica_groups=pairs, ins=[ta.ap().opt()], outs=[tb.ap().opt()]).then_inc(ccs))
            self.ccnt += 1
        c = self.ccnt
        kb.ops["pool"].append(lambda eng, c=c: eng.wait_ge(ccs, c))
        kb.run_stage("xchg")

    def dscr(self, name, shape, dt=F32):
        if name not in self.dr:
            self.dr[name] = self.nc.dram_tensor(name, list(shape), dt, kind="Internal").ap()
        return self.dr[name]

    def build(self):
        nc = self.nc
        with ExitStack() as st:
            self._stack = st
            self.kb = KB(nc, st)
            for ph, l in self.phases:
                getattr(self, "ph_" + ph)(l)
        return nc

    def _un(self, name):
        self._uid = getattr(self, "_uid", 0) + 1
        return "%s_%d" % (name, self._uid)

    def _sb(self, st, name, shape, dt, dma=True):
        name = self._un(name)
        t = st.enter_context(self.nc.sbuf_tensor(name, list(shape), dt))
        return self.kb.buf(t, dma=dma, name=name)

    def _ps(self, st, name, shape=(128, 512), dt=F32):
        name = self._un(name)
        t = st.enter_context(self.nc.psum_tensor(name, list(shape), dt))
        return self.kb.buf(t, dma=False, name=name)

    def _ident(self, st):
        kb = self.kb
        idin = self.din("ident", (128, 128))
        idf = self._sb(st, "idf", (128, 128), F32)
        idb = self._sb(st, "idb", (128, 128), BF16)
        kb.dma("sp", idf[:], idin, writes=[idf])
        kb.op("dve", lambda e: e.tensor_copy(out=idb[:], in_=idf[:]), reads=[idf], writes=[idb])
        return idf, idb

    def _rstd(self, var_ap, out_buf, tmp_buf, reads):
        kb = self.kb
        kb.op("dve", lambda e: e.tensor_scalar(out=tmp_buf[:], in0=var_ap, scalar1=EPS, scalar2=None, op0=ALU.add),
              reads=reads, writes=[tmp_buf])
        kb.op("act", lambda e: e.activation(out=tmp_buf[:], in_=tmp_buf[:], func=AF.Sqrt), reads=[tmp_buf], writes=[tmp_buf])
        kb.op("dve", lambda e: e.reciprocal(out=out_buf[:], in_=tmp_buf[:]), reads=[tmp_buf], writes=[out_buf])

    def _bcast_load(self, buf, row_ap):
        self.kb.dma("sp", buf[:], row_ap.partition_broadcast(128), writes=[buf])

    def _mods(self):
        return (self.din if self.mods_ext else self.dscr)("mods", (self.cfg.depth * 2, 6 * D))

    def _xres(self):
        return self.dscr("xres", (self.cfg.T, D))

    def ph_xin(self, l):
        cfg, kb = self.cfg, self.kb
        xin = self.din("x_in", (cfg.T, D))
        xres = self._xres()
        with ExitStack() as st:
            tb = [self._sb(st, "cp%d" % i, (128, D), F32) for i in range(2)]
            for i in range(cfg.NT):
                b = tb[i % 2]
                kb.dma("sp", b[:], xin[i * 128:(i + 1) * 128, :], writes=[b])
                kb.dma("sp", xres[i * 128:(i + 1) * 128, :], b[:], reads=[b])
            kb.run_stage("xin")

    def ph_xout(self, l):
        cfg, kb = self.cfg, self.kb
        xo = self.dout("x_out", (cfg.T, D))
        xres = self._xres()
        with ExitStack() as st:
            tb = [self._sb(st, "cp%d" % i, (128, D), F32) for i in range(2)]
            for i in range(cfg.NT):
                b = tb[i % 2]
                kb.dma("sp", b[:], xres[i * 128:(i + 1) * 128, :], writes=[b])
                kb.dma("sp", xo[i * 128:(i + 1) * 128, :], b[:], reads=[b])
            kb.run_stage("xout")

    def ph_norm0(self, l):
        cfg, kb = self.cfg, self.kb
        xin = self.din("x_raw", (cfg.T, D))
        xres = self._xres()
        with ExitStack() as st:
            xt = [self._sb(st, "xt%d" % i, (128, D), F32) for i in range(2)]
            stt = [self._sb(st, "stt%d" % i, (128, 4, 6), F32, dma=False) for i in range(2)]
            mv = [self._sb(st, "mv%d" % i, (128, 2), F32, dma=False) for i in range(2)]
            rs = [self._sb(st, "rs%d" % i, (128, 1), F32, dma=False) for i in range(2)]
            tm = [self._sb(st, "tm%d" % i, (128, 1), F32, dma=False) for i in range(2)]
            for i in range(cfg.NT):
                x, s_, m_, r_, t_ = xt[i % 2], stt[i % 2], mv[i % 2], rs[i % 2], tm[i % 2]
                kb.dma("sp", x[:], xin[i * 128:(i + 1) * 128, :], writes=[x])
                for q in range(4):
                    kb.op("dve", lambda e, q=q, x=x, s_=s_: e.bn_stats(out=s_[:, q, :], in_=x[:, q * 512:(q + 1) * 512]),
                          reads=[x], writes=[s_])
                kb.op("dve", lambda e, s_=s_, m_=m_: e.bn_aggr(out=m_[:], in_=s_[:].rearrange("p a b -> p (a b)")),
                      reads=[s_], writes=[m_])
                self._rstd(m_[:, 1:2], r_, t_, [m_])
                kb.op("dve", lambda e, x=x, m_=m_, r_=r_: e.tensor_scalar(out=x[:], in0=x[:], scalar1=m_[:, 0:1], scalar2=r_[:, 0:1],
                                                                          op0=ALU.subtract, op1=ALU.mult),
                      reads=[x, m_, r_], writes=[x])
                kb.dma("sp", xres[i * 128:(i + 1) * 128, :], x[:], reads=[x])
            kb.run_stage("norm0")

    def ph_mods(self, l):
        cfg, kb = self.cfg, self.kb
        cT = self.din("cT", (128, KC, 2))
        w_ada = self.din("w_ada", (cfg.depth, D, 6 * D))
        b_ada = self.din("b_ada", (cfg.depth, 6 * D))
        mods = self.dout("mods", (cfg.depth * 2, 6 * D)) if self.mods_ext else self.dscr("mods", (cfg.depth * 2, 6 * D))
        with ExitStack() as st:
            sc = self._sb(st, "sc", (128, KC, 2), F32)
            kb.dma("sp", sc[:], cT, writes=[sc])
            kb.op("act", lambda e: e.activation(out=sc[:], in_=sc[:], func=AF.Silu), reads=[sc], writes=[sc])
            scb = self._sb(st, "scb", (128, KC, 2), BF16, dma=False)
            kb.op("dve", lambda e: e.tensor_copy(out=scb[:], in_=sc[:]), reads=[sc], writes=[scb])
            wb = Rot([self._sb(st, "wa%d" % i, (128, 8, 512), F32) for i in range(4)])
            wq = Rot([self._sb(st, "wq%d" % i, (128, KC, 512), BF16, dma=False) for i in range(3)])
            bb = Rot([self._sb(st, "ba%d" % i, (2, 512), F32) for i in range(2)])
            ob = Rot([self._sb(st, "oa%d" % i, (2, 512), F32) for i in range(2)])
            pss = Rot([self._ps(st, "pm%d" % i, (2, 512)) for i in range(2)])
            cnt = 0
            for ll in range(cfg.depth):
                wl = w_ada[ll].rearrange("(kc p) n -> p kc n", p=128)
                for j in range(24):
                    wqt = wq.next()
                    for hh in range(2):
                        w = wb.next()
                        cnt += 1
                        kb.dma("sp" if cnt % 2 else "act", w[:], wl[:, hh * 8:(hh + 1) * 8, j * 512:(j + 1) * 512], writes=[w])
                        if cnt % 2:
                            kb.op("dve", lambda e, w=w, wqt=wqt, hh=hh: e.tensor_copy(out=wqt[:, hh * 8:(hh + 1) * 8, :], in_=w[:]), reads=[w], writes=[wqt])
                        else:
                            kb.op("act", lambda e, w=w, wqt=wqt, hh=hh: e.activation(out=wqt[:, hh * 8:(hh + 1) * 8, :], in_=w[:], func=AF.Identity), reads=[w], writes=[wqt])
                    b_ = bb.next()
                    kb.dma("sp", b_[:], b_ada[ll, j * 512:(j + 1) * 512].partition_broadcast(2), writes=[b_])
                    ps = pss.next()
                    kb.mm_group(ps, ps[:], [scb[:, k, :] for k in range(KC)], [wqt[:, k, :] for k in range(KC)], [scb, wqt])
                    o = ob.next()
                    kb.op("dve", lambda e, o=o, ps=ps, b_=b_: e.tensor_tensor(out=o[:], in0=ps[:], in1=b_[:], op=ALU.add),
                          reads=[ps, b_], writes=[o])
                    if j // 4 in (1, 4):
                        kb.op("dve", lambda e, o=o: e.tensor_scalar(out=o[:], in0=o[:], scalar1=1.0, scalar2=None, op0=ALU.add),
                              reads=[o], writes=[o])
                    kb.dma("sp", mods[ll * 2:ll * 2 + 2, j * 512:(j + 1) * 512], o[:], reads=[o])
            kb.run_stage("mods")

    def _make_hT(self, st, l, chunk_sh, chunk_sc, name):
        cfg, kb = self.cfg, self.kb
        xres = self._xres()
        mods = self._mods()
        if self.fused and getattr(self, "_hT_saved", None) is not None and self._hT_saved[0] == l:
            _, hT, hTb = self._hT_saved
            self._hT_saved = None
            return hT, hTb
        hT = st.enter_context(self.nc.sbuf_tensor(self._un("hT"), [128, KC, cfg.T], BF16))
        hTb = [kb.buf(hT, dma=True, name="hT%d" % i) for i in range(cfg.NT)]
        with ExitStack() as s2:
            idf, idb = self._ident(s2)
            A = [self._sb(s2, "mA%d" % s, (128, D), F32) for s in range(2)]
            Bv = [self._sb(s2, "mB%d" % s, (128, D), F32) for s in range(2)]
            for s in range(2):
                r = l * 2 + s
                self._bcast_load(A[s], mods[r, chunk_sc * D:(chunk_sc + 1) * D])
                self._bcast_load(Bv[s], mods[r, chunk_sh * D:(chunk_sh + 1) * D])
            xt = Rot([self._sb(s2, "hx%d" % i, (128, D), F32) for i in range(3)])
            tmp = Rot([self._sb(s2, "ht%d" % i, (128, D), F32, dma=False) for i in range(2)])
            hb = Rot([self._sb(s2, "hb%d" % i, (128, D), BF16, dma=False) for i in range(2)])
            ptr = Rot([self._ps(s2, "ptr%d" % i, (128, 4, 128), BF16) for i in range(4)])
            for i in range(cfg.NT):
                s = 0 if i < cfg.NTL else 1
                x, t_, h_ = xt.next(), tmp.next(), hb.next()
                kb.dma("sp" if i % 2 == 0 else "act", x[:], xres[i * 128:(i + 1) * 128, :], writes=[x])
                kb.op("dve", lambda e, x=x, t_=t_, s=s: e.tensor_tensor(out=t_[:], in0=x[:], in1=A[s][:], op=ALU.mult),
                      reads=[x, A[s]], writes=[t_])
                if i % 3 == 2:
                    kb.op("pool", lambda e, h_=h_, t_=t_, s=s: e.tensor_tensor(out=h_[:], in0=t_[:], in1=Bv[s][:], op=ALU.add),
                          reads=[t_, Bv[s]], writes=[h_])
                else:
                    kb.op("dve", lambda e, h_=h_, t_=t_, s=s: e.tensor_tensor(out=h_[:], in0=t_[:], in1=Bv[s][:], op=ALU.add),
                          reads=[t_, Bv[s]], writes=[h_])
                self._transpose_tile(h_, idb, ptr, hT, hTb[i], i)
            kb.run_stage(name)
        return hT, hTb

    def _transpose_tile(self, h_, idb, ptr, hT, hTbuf, i, all_act=False):
        kb = self.kb
        for q in range(4):
            pt = ptr.next()
            for r in range(4):
                kc = q * 4 + r
                kb.op("pe", lambda e, pt=pt, r=r, kc=kc, h_=h_: e.transpose(out=pt[:, r, :], in_=h_[:, kc * 128:(kc + 1) * 128],
                                                                         identity=idb[:]),
                      reads=[h_, idb], writes=[pt], inc=(r == 3))
            oap = hT[:, q * 4:(q + 1) * 4, i * 128:(i + 1) * 128]
            if q % 2 == 0 or all_act:
                kb.op("act", lambda e, pt=pt, oap=oap: e.activation(out=oap, in_=pt[:], func=AF.Identity), reads=[pt], writes=[hTbuf])
            else:
                kb.op("dve", lambda e, pt=pt, oap=oap: e.tensor_copy(out=oap, in_=pt[:]), reads=[pt], writes=[hTbuf])

    def _hread(self, hTb, t0, n):
        return hTb[t0 // 128:(t0 + n + 127) // 128]

    def _wload(self, wbuf, src3):
        kc = src3.shape[1]
        h = kc // 2
        self.kb.dma("pool", wbuf[:, 0:h, 0:src3.shape[2]], src3[:, 0:h, :], writes=[wbuf])
        self.kb.dma("pool", wbuf[:, h:kc, 0:src3.shape[2]], src3[:, h:kc, :], writes=[wbuf])

    def _wload_hw(self, stg, dst_ap, dst_buf, src_ap, n_el_shape=None):
        kb = self.kb
        sb_ = stg.next()
        self._hwq = getattr(self, "_hwq", 0) + 1
        qs = getattr(self, "_hw_queues", ("sp",))
        q = qs[self._hwq % len(qs)]
        shp = src_ap.shape
        if len(shp) == 3:
            sv = sb_[:, 0:shp[1], 0:shp[2]]
        else:
            sv = sb_[:, 0:shp[1]]
        kb.dma(q, sv, src_ap, writes=[sb_])
        engs = getattr(self, "_cast_engs", ("pool", "act"))
        ce = engs[self._hwq % len(engs)]
        if ce == "act":
            kb.op("act", lambda e: e.activation(out=dst_ap, in_=sv, func=AF.Identity), reads=[sb_], writes=[dst_buf])
        else:
            kb.op(ce, lambda e: e.tensor_copy(out=dst_ap, in_=sv), reads=[sb_], writes=[dst_buf])

    def _wload2(self, stg, wbuf, src3):
        kc = src3.shape[1]
        h = kc // 2
        n = src3.shape[2]
        self._wload_hw(stg, wbuf[:, 0:h, 0:n], wbuf, src3[:, 0:h, :])
        self._wload_hw(stg, wbuf[:, h:kc, 0:n], wbuf, src3[:, h:kc, :])

    def _pipeline(self, tasks):
        tasks[0][0]()
        for i, (ld, cp) in enumerate(tasks):
            if i + 1 < len(tasks):
                tasks[i + 1][0]()
            cp()

    def _rope_proj(self, st, wA, wB, jj, hT, hTb, blk, cs, sn, pss, tmps, out_ap, out_buf):
        kb = self.kb
        t0, bs = blk
        pA, pB = pss.next(), pss.next()
        hr = self._hread(hTb, t0, bs)
        kb.mm_group(pA, pA[:, 0:bs], [wA[:, k, jj * 128:(jj + 1) * 128] for k in range(KC)],
                    [hT[:, k, t0:t0 + bs] for k in range(KC)], [wA] + hr)
        kb.mm_group(pB, pB[:, 0:bs], [wB[:, k, jj * 128:(jj + 1) * 128] for k in range(KC)],
                    [hT[:, k, t0:t0 + bs] for k in range(KC)], [wB] + hr)
        t1, t2 = tmps.next(), tmps.next()
        kb.op("dve", lambda e: e.tensor_tensor(out=t1[:, 0:bs], in0=pA[:, 0:bs], in1=cs[:, t0:t0 + bs], op=ALU.mult),
              reads=[pA, cs], writes=[t1])
        kb.op("dve", lambda e: e.tensor_tensor(out=t2[:, 0:bs], in0=pB[:, 0:bs], in1=sn[:, t0:t0 + bs], op=ALU.mult),
              reads=[pB, sn], writes=[t2])
        kb.op("pool", lambda e: e.tensor_tensor(out=out_ap, in0=t1[:, 0:bs], in1=t2[:, 0:bs], op=ALU.add),
              reads=[t1, t2], writes=[out_buf])

    def ph_A(self, l):
        cfg, kb = self.cfg, self.kb
        T = cfg.T
        w_in = self.din("w_in", (self.wdep, D, IN_W))
        w_qkp = self.din("w_qkp", (self.wdep, D, 2048))
        ropec = self.din("rope_cos", (128, T))
        ropes = self.din("rope_sin", (128, T))
        mk = self.dout if self.xchg_ext else self.dcc
        KTx = [mk("KT_x%d" % h, (128, T), BF16) for h in range(8)]
        Vx = [mk("V_x%d" % h, (T, 128), BF16) for h in range(8)]
        Hx = mk("halo_x", (8 * 128, 32), F32).rearrange("(h p) t -> h p t", p=128)
        wl = w_in[self.li(l)].rearrange("(kc p) n -> p kc n", p=128)
        wpl = w_qkp[self.li(l)].rearrange("(kc p) n -> p kc n", p=128)
        with ExitStack() as st:
            if self.fused:
                self._hT_stack = ExitStack()
                hT, hTb = self._make_hT(self._hT_stack, l, 0, 1, "hT_A")
                self._hT_keep = (l, hT, hTb)
            else:
                hT, hTb = self._make_hT(st, l, 0, 1, "hT_A")
            cs = self._sb(st, "cs", (128, T), F32)
            sn = self._sb(st, "sn", (128, T), F32)
            kb.dma("sp", cs[:], ropec, writes=[cs])
            kb.dma("sp", sn[:], ropes, writes=[sn])
            wg = Rot([self._sb(st, "wg%d" % i, (128, KC, 512), BF16) for i in range(4)])
            stg = Rot([self._sb(st, "stg%d" % i, (128, 8, 512), F32) for i in range(2)])
            pss = Rot([self._ps(st, "ps%d" % i) for i in range(8)])
            tmps = Rot([self._sb(st, "rt%d" % i, (128, 512), F32, dma=False) for i in range(4)])
            kt = Rot([self._sb(st, "kt%d" % i, (128, 512), BF16) for i in range(3)])
            vt = Rot([self._sb(st, "vt%d" % i, (128, 512), BF16) for i in range(3)])
            hl = Rot([self._sb(st, "hl%d" % i, (128, 32), F32) for i in range(2)])
            ranges = [0, cfg.LAT - 8, cfg.LAT, T - 8]
            tasks = []
            for g in range(2):
                hold = {}

                def ldK(g=g, hold=hold):
                    hold["A"], hold["B"] = wg.next(), wg.next()
                    self._wload2(stg, hold["A"], wl[:, :, K_OFF + g * 512:K_OFF + (g + 1) * 512])
                    self._wload2(stg, hold["B"], wpl[:, :, 1024 + g * 512:1024 + (g + 1) * 512])

                def cpK(g=g, hold=hold):
                    wA, wB = hold["A"], hold["B"]
                    for jj in range(4):
                        h = g * 4 + jj
                        for blk in cfg.blocks:
                            t0, bs = blk
                            o = kt.next()
                            self._rope_proj(st, wA, wB, jj, hT, hTb, blk, cs, sn, pss, tmps, o[:, 0:bs], o)
                            kb.dma("sp", KTx[h][:, t0:t0 + bs], o[:, 0:bs], reads=[o])
                tasks.append((ldK, cpK))
            for g in range(2):
                hold = {}

                def ldV(g=g, hold=hold):
                    hold["v"] = wg.next()
                    self._wload2(stg, hold["v"], wl[:, :, V_OFF + g * 512:V_OFF + (g + 1) * 512])

                def cpV(g=g, hold=hold):
                    wv = hold["v"]
                    for i in range(cfg.NT):
                        ps = pss.next()
                        kb.mm_group(ps, ps[:], [hT[:, k, i * 128:(i + 1) * 128] for k in range(KC)],
                                    [wv[:, k, :] for k in range(KC)], [wv, hTb[i]])
                        o = vt.next()
                        kb.op("act", lambda e, o=o, ps=ps: e.activation(out=o[:], in_=ps[:], func=AF.Identity), reads=[ps], writes=[o])
                        for hh in range(4):
                            kb.dma("sp", Vx[g * 4 + hh][i * 128:(i + 1) * 128, :], o[:, hh * 128:(hh + 1) * 128], reads=[o])
                tasks.append((ldV, cpV))
            for g in range(2):
                hold = {}

                def ldC(g=g, hold=hold):
                    hold["c"] = wg.next()
                    self._wload2(stg, hold["c"], wl[:, :, C_OFF + g * 512:C_OFF + (g + 1) * 512])

                def cpC(g=g, hold=hold):
                    wc = hold["c"]
                    for jj in range(4):
                        ps = pss.next()
                        for r, t0 in enumerate(ranges):
                            kb.mm_group(ps, ps[:, r * 8:(r + 1) * 8], [wc[:, k, jj * 128:(jj + 1) * 128] for k in range(KC)],
                                        [hT[:, k, t0:t0 + 8] for k in range(KC)], [wc] + self._hread(hTb, t0, 8))
                        o = hl.next()
                        kb.op("dve", lambda e, o=o, ps=ps: e.tensor_copy(out=o[:], in_=ps[:, 0:32]), reads=[ps], writes=[o])
                        kb.dma("sp", Hx[g * 4 + jj, :, :], o[:], reads=[o])
                tasks.append((ldC, cpC))
            self._pipeline(tasks)
            kb.run_stage("A")

    def ph_B(self, l):
        cfg = self.cfg
        last = (l == cfg.depth - 1)
        self.blocksB = cfg.lat_blocks if last else cfg.blocks
        self.tilesB = cfg.NTL if last else cfg.NT
        with ExitStack() as st:
            if self.fused and getattr(self, "_hT_keep", None) is not None and self._hT_keep[0] == l:
                _, hT, hTb = self._hT_keep
                self._hT_keep = None
                st.enter_context(self._hT_stack)
            else:
                hT, hTb = self._make_hT(st, l, 0, 1, "hT_B")
            self._attn(st, l, hT, hTb, last)
            self._gmlp(st, l, hT, hTb)
            self._poolbr(st, l, hT, hTb)
            self._merge(st, l, hT, hTb)
        self._outproj(l, last)
        self._ffn_up(l)
        self._ffn_down(l, last)

    def _yT(self):
        return self.dscr("yT", (3, 8, 128, self.cfg.T), BF16)

    def _gathered(self):
        cfg = self.cfg
        mk = self.din if self.xchg_ext else self.dcc
        return ([mk("KT_all%d" % h, (2 * 128, cfg.T), BF16).rearrange("(r p) t -> r p t", r=2) for h in range(8)],
                [mk("V_all%d" % h, (2 * cfg.T, 128), BF16).rearrange("(r t) d -> r t d", r=2) for h in range(8)],
                mk("halo_all", (2 * 8 * 128, 32), F32).rearrange("(r h p) t -> r h p t", r=2, p=128))

    def _attn(self, st0, l, hT, hTb, last):
        cfg, kb = self.cfg, self.kb
        T, NT = cfg.T, cfg.NT
        lam_init = 0.8 - 0.6 * math.exp(-0.3 * l)
        w_in = self.din("w_in", (self.wdep, D, IN_W))
        w_qkp = self.din("w_qkp", (self.wdep, D, 2048))
        ropec = self.din("rope_cos", (128, T))
        ropes = self.din("rope_sin", (128, T))
        lamqk = self.din("lam_qk", (self.wdep, 256))
        sublng = self.din("subln_g", (self.wdep, 128, 1))
        KTa, Va, _ = self._gathered()
        yT = self._yT()
        wl = w_in[self.li(l)].rearrange("(kc p) n -> p kc n", p=128)
        wpl = w_qkp[self.li(l)].rearrange("(kc p) n -> p kc n", p=128)
        with ExitStack() as st:
            cs = self._sb(st, "cs", (128, T), F32)
            sn = self._sb(st, "sn", (128, T), F32)
            kb.dma("sp", cs[:], ropec, writes=[cs])
            kb.dma("sp", sn[:], ropes, writes=[sn])
            lq = self._sb(st, "lq", (128, 4, 64), F32)
            kb.dma("sp", lq[:], lamqk[self.li(l), :].partition_broadcast(128).rearrange("p (a b) -> p a b", a=4), writes=[lq])
            lp = self._sb(st, "lp", (128, 2, 64), F32, dma=False)
            ls = self._sb(st, "ls", (128, 2), F32, dma=False)
            nlam = self._sb(st, "nlam", (128, 1), F32, dma=False)
            kb.op("dve", lambda e: e.tensor_tensor(out=lp[:, 0, :], in0=lq[:, 0, :], in1=lq[:, 1, :], op=ALU.mult), reads=[lq], writes=[lp])
            kb.op("dve", lambda e: e.tensor_tensor(out=lp[:, 1, :], in0=lq[:, 2, :], in1=lq[:, 3, :], op=ALU.mult), reads=[lq, lp], writes=[lp])
            kb.op("dve", lambda e: e.tensor_reduce(out=ls[:], in_=lp[:], axis=mybir.AxisListType.X, op=ALU.add), reads=[lp], writes=[ls])
            kb.op("act", lambda e: e.activation(out=ls[:], in_=ls[:], func=AF.Exp), reads=[ls], writes=[ls])
            kb.op("dve", lambda e: e.tensor_tensor(out=nlam[:], in0=ls[:, 1:2], in1=ls[:, 0:1], op=ALU.subtract), reads=[ls], writes=[nlam])
            kb.op("dve", lambda e: e.tensor_scalar(out=nlam[:], in0=nlam[:], scalar1=-lam_init, scalar2=None, op0=ALU.add),
                  reads=[nlam], writes=[nlam])
            gsub = self._sb(st, "gsub", (128, 1), F32)
            kb.dma("sp", gsub[:], sublng[self.li(l)], writes=[gsub])
            kb.op("dve", lambda e: e.tensor_scalar(out=gsub[:], in0=gsub[:], scalar1=(1.0 - lam_init), scalar2=None, op0=ALU.mult),
                  reads=[gsub], writes=[gsub])
            onesb = self._sb(st, "onesb", (128, 128), BF16, dma=False)
            onesf = self._sb(st, "onesf", (128, 128), F32, dma=False)
            kb.op("pool", lambda e: e.memset(onesb[:], 1.0), writes=[onesb])
            kb.op("pool", lambda e: e.memset(onesf[:], 1.0), writes=[onesf])

            wg = Rot([self._sb(st, "wg%d" % i, (128, KC, 512), BF16) for i in range(2)])
            qT = Rot([self._sb(st, "qT%d" % i, (128, T), BF16, dma=False) for i in range(2)])
            ktb = Rot([self._sb(st, "ktb%d" % i, (128, 2, T), BF16) for i in range(2)])
            vb = Rot([self._sb(st, "vb%d" % i, (128, 2 * NT, 128), BF16) for i in range(2)])
            pst2 = Rot([self._ps(st, "pst2_%d" % i, (128, 1024)) for i in range(2)])
            pacc = [self._ps(st, "pacc%d" % i) for i in range(4)]
            pst = pst2
            E2 = Rot([self._sb(st, "E%d" % i, (128, 1024), BF16, dma=False) for i in range(3)])
            tmps = Rot([self._sb(st, "rt%d" % i, (128, 512), F32, dma=False) for i in range(4)])
            ep = [self._sb(st, "ep%d" % i, (128, 512), F32, dma=False) for i in range(4)]
            yab = Rot([self._sb(st, "yab%d" % i, (128, 512), BF16) for i in range(2)])
            wA = wB = None
            pend = [None]
            for h in range(8):
                g, jj = h // 4, h % 4
                if jj == 0:
                    wA, wB = wg.next(), wg.next()
                    self._wload(wA, wl[:, :, g * 512:(g + 1) * 512])
                    self._wload(wB, wpl[:, :, g * 512:(g + 1) * 512])
                q = qT.next()
                for blk in self.blocksB:
                    t0, bs = blk
                    self._rope_proj(st, wA, wB, jj, hT, hTb, blk, cs, sn, pst, tmps, q[:, t0:t0 + bs], q)
                kt_, v_ = ktb.next(), vb.next()
                for r in range(2):
                    kb.dma("sp", kt_[:, r, :], KTa[h][r, :, :], writes=[kt_])
                    kb.dma("sp", v_[:, r * NT:(r + 1) * NT, :],
                           Va[h][r, :, :].rearrange("(i p) d -> p i d", p=128), writes=[v_])
                for blk in self.blocksB:
                    t0, bs = blk
                    is_ctx = t0 >= cfg.LAT
                    if is_ctx:
                        chunks = [(0, cfg.NTL), (1, cfg.NTL)]
                    else:
                        chunks = [(r, i) for r in range(2) for i in range(NT)]
                    nck = len(chunks)

                    def issue_S(ci, kt_=kt_, q=q, t0=t0, bs=bs, chunks=chunks):
                        r, i = chunks[ci]
                        sp2 = pst2.next()
                        for m in range(2):
                            kb.op("pe", lambda e, sp2=sp2, m=m, r=r, i=i:
                                  e.matmul(sp2[:, m * 512:m * 512 + bs], lhsT=kt_[m * 64:(m + 1) * 64, r, i * 128:(i + 1) * 128],
                                           rhs=q[m * 64:(m + 1) * 64, t0:t0 + bs], start=True, stop=True),
                                  reads=[kt_, q], writes=[sp2], inc=(m == 1))
                        return sp2

                    sp_next = issue_S(0)
                    for ci, (r, i) in enumerate(chunks):
                        kidx = r * NT + i
                        sp2 = sp_next
                        E = E2.next()
                        kb.op("act", lambda e, E=E, sp2=sp2, bs=bs: e.activation(
                            out=E[:].rearrange("p (m n) -> p m n", m=2)[:, :, 0:bs],
                            in_=sp2[:].rearrange("p (m n) -> p m n", m=2)[:, :, 0:bs], func=AF.Exp, scale=A_SCALE),
                            reads=[sp2], writes=[E])
                        if ci + 1 < nck:
                            sp_next = issue_S(ci + 1)
                        if ci == 2 and pend[0] is not None:
                            pend[0]()
                            pend[0] = None
                        for m in range(2):
                            kb.op("pe", lambda e, m=m, v_=v_, kidx=kidx, E=E, bs=bs, ci=ci, nck=nck:
                                  e.matmul(pacc[m][:, 0:bs], lhsT=v_[:, kidx, :], rhs=E[:, m * 512:m * 512 + bs], start=(ci == 0), stop=(ci == nck - 1)),
                                  reads=[v_, E], writes=[pacc[m]], inc=False)
                            kb.op("pe", lambda e, m=m, E=E, bs=bs, ci=ci, nck=nck:
                                  e.matmul(pacc[2 + m][:, 0:bs], lhsT=onesb[:], rhs=E[:, m * 512:m * 512 + bs], start=(ci == 0), stop=(ci == nck - 1)),
                                  reads=[onesb, E], writes=[pacc[2 + m]], inc=(m == 1))
                    if pend[0] is not None:
                        pend[0]()
                        pend[0] = None
                    r0, r1, a0, a1 = ep
                    kb.op("dve", lambda e, bs=bs: e.reciprocal(out=r0[:, 0:bs], in_=pacc[2][:, 0:bs]), reads=[pacc[2]], writes=[r0])
                    kb.op("dve", lambda e, bs=bs: e.reciprocal(out=r1[:, 0:bs], in_=pacc[3][:, 0:bs]), reads=[pacc[3]], writes=[r1])
                    kb.op("dve", lambda e, bs=bs: e.tensor_tensor(out=a0[:, 0:bs], in0=pacc[0][:, 0:bs], in1=r0[:, 0:bs], op=ALU.mult),
                          reads=[pacc[0], r0], writes=[a0])
                    kb.op("dve", lambda e, bs=bs: e.tensor_tensor(out=a1[:, 0:bs], in0=pacc[1][:, 0:bs], in1=r1[:, 0:bs], op=ALU.mult),
                          reads=[pacc[1], r1], writes=[a1])
                    kb.op("dve", lambda e, bs=bs: e.scalar_tensor_tensor(out=a0[:, 0:bs], in0=a1[:, 0:bs], scalar=nlam[:, 0:1], in1=a0[:, 0:bs],
                                                                        op0=ALU.mult, op1=ALU.add),
                          reads=[a1, nlam, a0], writes=[a0])
                    kb.op("pool", lambda e, bs=bs: e.tensor_tensor(out=a1[:, 0:bs], in0=a0[:, 0:bs], in1=a0[:, 0:bs], op=ALU.mult),
                          reads=[a0], writes=[a1])

                    def part2(bs=bs, t0=t0, h=h):
                        sps = pst.next()
                        kb.op("pe", lambda e, sps=sps, bs=bs: e.matmul(sps[:, 0:bs], lhsT=onesf[:], rhs=a1[:, 0:bs], start=True, stop=True),
                              reads=[onesf, a1], writes=[sps])
                        kb.op("dve", lambda e, sps=sps, bs=bs: e.tensor_scalar(out=r0[:, 0:bs], in0=sps[:, 0:bs], scalar1=1.0 / 128, scalar2=EPS,
                                                                               op0=ALU.mult, op1=ALU.add), reads=[sps], writes=[r0])
                        kb.op("act", lambda e, bs=bs: e.activation(out=r0[:, 0:bs], in_=r0[:, 0:bs], func=AF.Sqrt), reads=[r0], writes=[r0])
                        kb.op("dve", lambda e, bs=bs: e.reciprocal(out=r1[:, 0:bs], in_=r0[:, 0:bs]), reads=[r0], writes=[r1])
                        kb.op("dve", lambda e, bs=bs: e.tensor_tensor(out=a0[:, 0:bs], in0=a0[:, 0:bs], in1=r1[:, 0:bs], op=ALU.mult),
                              reads=[a0, r1], writes=[a0])
                        yo = yab.next()
                        kb.op("dve", lambda e, yo=yo, bs=bs: e.tensor_scalar(out=yo[:, 0:bs], in0=a0[:, 0:bs], scalar1=gsub[:, 0:1], scalar2=None, op0=ALU.mult),
                              reads=[a0, gsub], writes=[yo])
                        kb.dma("sp", yT[0, h, :, t0:t0 + bs], yo[:, 0:bs], reads=[yo])
                    pend[0] = part2
            if pend[0] is not None:
                pend[0]()
                pend[0] = None
            kb.run_stage("attn")

    def _gmlp(self, st0, l, hT, hTb):
        cfg, kb = self.cfg, self.kb
        T = cfg.T
        w_in = self.din("w_in", (self.wdep, D, IN_W))
        wsT = self.din("wsT", (self.wdep, 128, 8, 128))
        bsp = self.din("b_spatial", (self.wdep, 1024))
        lng = self.din("gmlp_ln_g", (self.wdep, 1024))
        lnb = self.din("gmlp_ln_b", (self.wdep, 1024))
        yT = self._yT()
        wl = w_in[self.li(l)].rearrange("(kc p) n -> p kc n", p=128)
        with ExitStack() as st:
            wu = self._sb(st, "wu", (128, KC, 1024), BF16)
            wv = self._sb(st, "wv", (128, KC, 1024), BF16)
            wug = [wu, kb.buf(wu.t, dma=True, name="wu1")]
            wvg = [wv, kb.buf(wv.t, dma=True, name="wv1")]
            for g in range(2):
                kb.dma("pool", wu[:, :, g * 512:(g + 1) * 512], wl[:, :, BU_OFF + g * 512:BU_OFF + (g + 1) * 512], writes=[wug[g]])
            for g in range(2):
                kb.dma("pool", wv[:, :, g * 512:(g + 1) * 512], wl[:, :, BV_OFF + g * 512:BV_OFF + (g + 1) * 512], writes=[wvg[g]])
            ws = self._sb(st, "ws", (128, 8, 128), BF16)
            kb.dma("pool", ws[:], wsT[self.li(l)], writes=[ws])
            bsb = self._sb(st, "bsb", (128, 8, 128), F32)
            kb.dma("sp", bsb[:], bsp[self.li(l), :].partition_broadcast(128).rearrange("p (g i) -> p g i", g=8), writes=[bsb])
            gbc = self._sb(st, "gbc", (128, 1024), F32)
            bbc = self._sb(st, "bbc", (128, 1024), F32)
            self._bcast_load(gbc, lng[self.li(l), :])
            self._bcast_load(bbc, lnb[self.li(l), :])
            ub = Rot([self._sb(st, "ub%d" % i, (128, 8, 512), F32, dma=False) for i in range(1)])
            pss = Rot([self._ps(st, "ps%d" % i) for i in range(8)])
            vg = Rot([self._sb(st, "vg%d" % i, (128, 1024), F32, dma=False) for i in range(2)])
            vln = Rot([self._sb(st, "vln%d" % i, (128, 1024), BF16, dma=False) for i in range(3)])
            stt = Rot([self._sb(st, "stt%d" % i, (128, 2, 6), F32, dma=False) for i in range(2)])
            mv = Rot([self._sb(st, "mv%d" % i, (128, 2), F32, dma=False) for i in range(2)])
            rs = Rot([self._sb(st, "rs%d" % i, (128, 1), F32, dma=False) for i in range(2)])
            tm = Rot([self._sb(st, "tm%d" % i, (128, 1), F32, dma=False) for i in range(2)])
            stmp = Rot([self._sb(st, "stmp%d" % i, (128, 4, 128), F32, dma=False) for i in range(2)])
            ybt = Rot([self._sb(st, "ybt%d" % i, (128, 8, 128), BF16) for i in range(2)])
            for blk in self.blocksB:
                t0, bs = blk
                u = ub.next()
                hr = self._hread(hTb, t0, bs)
                for c8 in range(8):
                    ps = pss.next()
                    kb.mm_group(ps, ps[:, 0:bs], [wu[:, k, c8 * 128:(c8 + 1) * 128] for k in range(KC)],
                                [hT[:, k, t0:t0 + bs] for k in range(KC)], [wug[c8 // 4]] + hr)
                    kb.op("act", lambda e, u=u, c8=c8, ps=ps, bs=bs: e.activation(out=u[:, c8, 0:bs], in_=ps[:, 0:bs], func=AF.Gelu),
                          reads=[ps], writes=[u])
                def stA(ti, t0=t0):
                    i = t0 // 128 + ti
                    v_ = vg.next()
                    for n in range(2):
                        ps = pss.next()
                        kb.mm_group(ps, ps[:], [hT[:, k, i * 128:(i + 1) * 128] for k in range(KC)],
                                    [wv[:, k, n * 512:(n + 1) * 512] for k in range(KC)], [wvg[n], hTb[i]])
                        kb.op("act", lambda e, v_=v_, n=n, ps=ps: e.activation(out=v_[:, n * 512:(n + 1) * 512], in_=ps[:], func=AF.Gelu),
                              reads=[ps], writes=[v_])
                    s_, m_, r_, t_ = stt.next(), mv.next(), rs.next(), tm.next()
                    for n in range(2):
                        kb.op("dve", lambda e, s_=s_, v_=v_, n=n: e.bn_stats(out=s_[:, n, :], in_=v_[:, n * 512:(n + 1) * 512]),
                              reads=[v_], writes=[s_])
                    kb.op("dve", lambda e, s_=s_, m_=m_: e.bn_aggr(out=m_[:], in_=s_[:].rearrange("p a b -> p (a b)")), reads=[s_], writes=[m_])
                    self._rstd(m_[:, 1:2], r_, t_, [m_])
                    kb.op("dve", lambda e, v_=v_, m_=m_, r_=r_: e.tensor_scalar(out=v_[:], in0=v_[:], scalar1=m_[:, 0:1], scalar2=r_[:, 0:1],
                                                                              op0=ALU.subtract, op1=ALU.mult), reads=[v_, m_, r_], writes=[v_])
                    kb.op("dve", lambda e, v_=v_: e.tensor_tensor(out=v_[:], in0=v_[:], in1=gbc[:], op=ALU.mult), reads=[v_, gbc], writes=[v_])
                    vl = vln.next()
                    kb.op("pool", lambda e, v_=v_, vl=vl: e.tensor_tensor(out=vl[:], in0=v_[:], in1=bbc[:], op=ALU.add), reads=[v_, bbc], writes=[vl])
                    return vl

                def stB(ti, vl, t0=t0, u=u):
                    i = t0 // 128 + ti
                    yb = ybt.next()
                    for half in range(2):
                        ps = pss.next()
                        for gg in range(4):
                            g8 = half * 4 + gg
                            kb.op("pe", lambda e, ps=ps, gg=gg, g8=g8, vl=vl: e.matmul(ps[:, gg * 128:(gg + 1) * 128], lhsT=vl[:, g8 * 128:(g8 + 1) * 128],
                                                                                     rhs=ws[:, g8, :], start=True, stop=True),
                                  reads=[vl, ws], writes=[ps], inc=(gg == 3))
                        sm = stmp.next()
                        kb.op("dve", lambda e, sm=sm, ps=ps, half=half: e.tensor_tensor(out=sm[:], in0=ps[:].rearrange("p (g i) -> p g i", g=4),
                                                                                        in1=bsb[:, half * 4:(half + 1) * 4, :], op=ALU.add),
                              reads=[ps, bsb], writes=[sm])
                        kb.op("pool", lambda e, sm=sm, yb=yb, u=u, half=half, ti=ti: e.tensor_tensor(
                            out=yb[:, half * 4:(half + 1) * 4, :], in0=sm[:], in1=u[:, half * 4:(half + 1) * 4, ti * 128:(ti + 1) * 128], op=ALU.mult),
                            reads=[sm, u], writes=[yb])
                    kb.dma("sp", yT[1, :, :, i * 128:(i + 1) * 128].rearrange("c p t -> p c t"), yb[:], reads=[yb])

                ntl = bs // 128
                vl_next = stA(0)
                for ti in range(ntl):
                    vl_cur = vl_next
                    if ti + 1 < ntl:
                        vl_next = stA(ti + 1)
                    stB(ti, vl_cur)
            kb.run_stage("gmlp")

    def _poolbr(self, st0, l, hT, hTb):
        cfg, kb = self.cfg, self.kb
        T, LAT = cfg.T, cfg.LAT
        w_in = self.din("w_in", (self.wdep, D, IN_W))
        w_pool = self.din("w_pool", (self.wdep, 4, 256, 256))
        pscale = self.din("pool_scaleT", (self.wdep, 128, 8))
        hmask = self.din("hmask", (128, 2))
        corr = self.din("pcorr", (128, 2, 4, 16))
        _, _, Ha = self._gathered()
        yT = self._yT()
        wl = w_in[self.li(l)].rearrange("(kc p) n -> p kc n", p=128)
        has_ctx = len(self.blocksB) > len(cfg.lat_blocks)
        with ExitStack() as st:
            wg = Rot([self._sb(st, "wg%d" % i, (128, KC, 512), BF16) for i in range(2)])
            wp = self._sb(st, "wp", (128, 4, 2, 256), BF16)
            kb.dma("pool", wp[:], w_pool[self.li(l)].rearrange("g (cc p) d -> p g cc d", p=128), writes=[wp])
            psc = self._sb(st, "psc", (128, 8), F32)
            kb.dma("sp", psc[:], pscale[self.li(l)], writes=[psc])
            hm = self._sb(st, "hm", (128, 2), F32)
            kb.dma("sp", hm[:], hmask, writes=[hm])
            cr = self._sb(st, "cr", (128, 2, 4, 16), F32)
            kb.dma("sp", cr[:], corr, writes=[cr])
            zc = [self._sb(st, "zc%d" % i, (128, LAT + 16), F32, dma=False) for i in range(2)]
            zx = [self._sb(st, "zx%d" % i, (128, 128 + 16), F32, dma=False) for i in range(2)]
            dT = [self._sb(st, "dT%d" % i, (128, T), BF16, dma=False) for i in range(2)]
            Aa = self._sb(st, "Aa", (128, LAT + 16), F32, dma=False)
            Ab = self._sb(st, "Ab", (128, LAT + 16), F32, dma=False)
            hl = Rot([self._sb(st, "hl%d" % i, (128, 2, 32), F32) for i in range(2)])
            pss = Rot([self._ps(st, "ps%d" % i) for i in range(8)])
            yo = Rot([self._sb(st, "yo%d" % i, (128, 512), BF16) for i in range(3)])
            wc = None
            for g4 in range(4):
                w = WINS[g4]
                if g4 % 2 == 0:
                    wc = wg.next()
                    self._wload(wc, wl[:, :, C_OFF + (g4 // 2) * 512:C_OFF + (g4 // 2 + 1) * 512])
                for cc in range(2):
                    c8 = 2 * g4 + cc
                    jj = c8 % 4
                    z, zz, d_ = zc[cc], zx[cc], dT[cc]
                    for blk in self.blocksB:
                        t0, bs = blk
                        ps = pss.next()
                        kb.mm_group(ps, ps[:, 0:bs], [wc[:, k, jj * 128:(jj + 1) * 128] for k in range(KC)],
                                    [hT[:, k, t0:t0 + bs] for k in range(KC)], [wc] + self._hread(hTb, t0, bs))
                        if t0 < LAT:
                            kb.op("act", lambda e, z=z, ps=ps, t0=t0, bs=bs: e.activation(out=z[:, 8 + t0:8 + t0 + bs], in_=ps[:, 0:bs], func=AF.Identity),
                                  reads=[ps], writes=[z])
                        else:
                            kb.op("act", lambda e, zz=zz, ps=ps: e.activation(out=zz[:, 8:136], in_=ps[:, 0:128], func=AF.Identity),
                                  reads=[ps], writes=[zz])
                    h_ = hl.next()
                    kb.dma("sp", h_[:], Ha[:, c8, :, :].rearrange("r p t -> p r t"), writes=[h_])
                    kb.op("dve", lambda e, z=z, h_=h_: e.tensor_scalar(out=z[:, 0:8], in0=h_[:, 0, 8:16], scalar1=hm[:, 0:1], scalar2=None, op0=ALU.mult),
                          reads=[h_, hm], writes=[z])
                    kb.op("dve", lambda e, z=z, h_=h_: e.tensor_scalar(out=z[:, 8 + LAT:16 + LAT], in0=h_[:, 1, 0:8], scalar1=hm[:, 1:2], scalar2=None, op0=ALU.mult),
                          reads=[h_, hm], writes=[z])
                    self._pool1d(z, LAT, w, g4, 0, Aa, Ab, cr, d_, 0)
                    if has_ctx:
                        kb.op("dve", lambda e, zz=zz, h_=h_: e.tensor_scalar(out=zz[:, 0:8], in0=h_[:, 0, 24:32], scalar1=hm[:, 0:1], scalar2=None, op0=ALU.mult),
                              reads=[h_, hm], writes=[zz])
                        kb.op("dve", lambda e, zz=zz, h_=h_: e.tensor_scalar(out=zz[:, 136:144], in0=h_[:, 1, 16:24], scalar1=hm[:, 1:2], scalar2=None, op0=ALU.mult),
                              reads=[h_, hm], writes=[zz])
                        self._pool1d(zz, 128, w, g4, 1, Aa, Ab, cr, d_, LAT)
                for dd in range(2):
                    co = 2 * g4 + dd
                    for blk in self.blocksB:
                        t0, bs = blk
                        ps = pss.next()
                        kb.mm_group(ps, ps[:, 0:bs], [wp[:, g4, cc, dd * 128:(dd + 1) * 128] for cc in range(2)],
                                    [dT[cc][:, t0:t0 + bs] for cc in range(2)], [wp, dT[0], dT[1]])
                        o = yo.next()
                        kb.op("act", lambda e, o=o, ps=ps, bs=bs, co=co: e.activation(out=o[:, 0:bs], in_=ps[:, 0:bs], func=AF.Identity, scale=psc[:, co:co + 1]),
                              reads=[ps, psc], writes=[o])
                        kb.dma("sp", yT[2, co, :, t0:t0 + bs], o[:, 0:bs], reads=[o])
            kb.run_stage("pool")

    def _pool1d(self, z, n, w, g4, seq, Aa, Ab, cr, d_, dcol):
        kb = self.kb
        src, length, step = z, n + 16, 1
        bufs = [Aa, Ab]
        bi = 0
        while step < w:
            dst = bufs[bi % 2]
            bi += 1
            nl = length - step
            kb.op("pool", lambda e, dst=dst, src=src, nl=nl, step=step: e.tensor_tensor(out=dst[:, 0:nl], in0=src[:, 0:nl], in1=src[:, step:step + nl], op=ALU.add),
                  reads=[src], writes=[dst])
            src, length, step = dst, nl, step * 2
        off = 8 - w // 2
        other = bufs[bi % 2]
        kb.op("dve", lambda e: e.tensor_scalar(out=other[:, 0:n], in0=src[:, off:off + n], scalar1=1.0 / w, scalar2=None, op0=ALU.mult),
              reads=[src], writes=[other])
        kb.op("dve", lambda e: e.tensor_tensor(out=other[:, 0:8], in0=other[:, 0:8], in1=cr[:, seq, g4, 0:8], op=ALU.mult),
              reads=[other, cr], writes=[other])
        kb.op("dve", lambda e: e.tensor_tensor(out=other[:, n - 8:n], in0=other[:, n - 8:n], in1=cr[:, seq, g4, 8:16], op=ALU.mult),
              reads=[other, cr], writes=[other])
        kb.op("dve", lambda e: e.tensor_tensor(out=d_[:, dcol:dcol + n], in0=other[:, 0:n], in1=z[:, 8:8 + n], op=ALU.subtract),
              reads=[other, z], writes=[d_])

    def _merge(self, st0, l, hT, hTb):
        cfg, kb = self.cfg, self.kb
        T = cfg.T
        w_in = self.din("w_in", (self.wdep, D, IN_W))
        w_br = self.din("w_branch", (self.wdep, 3, 1024, D))
        yT = self._yT()
        mT = self.dscr("mT", (KC, 128, T), BF16)
        wl = w_in[self.li(l)].rearrange("(kc p) n -> p kc n", p=128)
        wbl = w_br[self.li(l)].rearrange("n (kc p) d -> p n kc d", p=128)
        nb = len(self.blocksB)
        sbs = [self.blocksB[:max(1, nb // 2)], self.blocksB[max(1, nb // 2):]]
        sbs = [s for s in sbs if s]
        maxw = max(sum(b[1] for b in s) for s in sbs)
        with ExitStack() as st:
            ysb = self._sb(st, "ysb", (128, 3, 8, maxw), BF16)
            wb = Rot([self._sb(st, "wb%d" % i, (128, 3, 8, 128), BF16) for i in range(2)])
            wgt = Rot([self._sb(st, "wgt%d" % i, (128, 3, KC, 128), BF16) for i in range(2)])
            pss = Rot([self._ps(st, "ps%d" % i) for i in range(8)])
            sg = Rot([self._sb(st, "sg%d" % i, (128, 512), F32, dma=False) for i in range(3)])
            pr = Rot([self._sb(st, "pr%d" % i, (128, 512), F32, dma=False) for i in range(3)])
            mo = Rot([self._sb(st, "mo%d" % i, (128, 512), BF16) for i in range(3)])
            stgg = Rot([self._sb(st, "stgg%d" % i, (128, KC, 128), F32) for i in range(2)])
            stgb = Rot([self._sb(st, "stgb%d" % i, (128, 8, 128), F32) for i in range(2)])
            self._cast_engs = ("act", "dve")
            self._hw_queues = ("sp", "sp", "act")
            for sbk in sbs:
                s0 = sbk[0][0]
                sw = sum(b[1] for b in sbk)
                for n in range(3):
                    kb.dma("sp", ysb[:, n, :, 0:sw], yT[n, :, :, s0:s0 + sw].rearrange("c p t -> p c t"), writes=[ysb])
                tasks = []
                for dc in range(KC):
                    hold = {}

                    def ld(dc=dc, hold=hold):
                        hold["b"], hold["g"] = wb.next(), wgt.next()
                        for n in range(3):
                            self._wload_hw(stgb, hold["b"][:, n, :, :], hold["b"], wbl[:, n, :, dc * 128:(dc + 1) * 128])
                            c0 = G_OFF + n * D + dc * 128
                            self._wload_hw(stgg, hold["g"][:, n, :, :], hold["g"], wl[:, :, c0:c0 + 128])

                    def cp(dc=dc, hold=hold, sbk=sbk, s0=s0):
                        wb_, wg_ = hold["b"], hold["g"]
                        for blk in sbk:
                            t0, bs = blk
                            lo = t0 - s0
                            hr = self._hread(hTb, t0, bs)
                            prods = []
                            for n in range(3):
                                pg, pp = pss.next(), pss.next()
                                kb.mm_group(pg, pg[:, 0:bs], [wg_[:, n, k, :] for k in range(KC)], [hT[:, k, t0:t0 + bs] for k in range(KC)], [wg_] + hr)
                                kb.mm_group(pp, pp[:, 0:bs], [wb_[:, n, k, :] for k in range(8)], [ysb[:, n, k, lo:lo + bs] for k in range(8)], [wb_, ysb])
                                s_ = sg.next()
                                kb.op("act", lambda e, s_=s_, pg=pg, bs=bs: e.activation(out=s_[:, 0:bs], in_=pg[:, 0:bs], func=AF.Sigmoid), reads=[pg], writes=[s_])
                                p_ = pr.next()
                                kb.op("dve", lambda e, p_=p_, s_=s_, pp=pp, bs=bs: e.tensor_tensor(out=p_[:, 0:bs], in0=pp[:, 0:bs], in1=s_[:, 0:bs], op=ALU.mult),
                                      reads=[pp, s_], writes=[p_])
                                prods.append(p_)
                            p0, p1, p2 = prods
                            kb.op("pool", lambda e, p0=p0, p1=p1, bs=bs: e.tensor_tensor(out=p0[:, 0:bs], in0=p0[:, 0:bs], in1=p1[:, 0:bs], op=ALU.add),
                                  reads=[p0, p1], writes=[p0])
                            o = mo.next()
                            kb.op("pool", lambda e, o=o, p0=p0, p2=p2, bs=bs: e.tensor_tensor(out=o[:, 0:bs], in0=p0[:, 0:bs], in1=p2[:, 0:bs], op=ALU.add),
                                  reads=[p0, p2], writes=[o])
                            kb.dma("sp", mT[dc, :, t0:t0 + bs], o[:, 0:bs], reads=[o])
                    tasks.append((ld, cp))
                self._pipeline(tasks)
            self._cast_engs = ("pool", "act")
            self._hw_queues = ("sp",)
            kb.run_stage("merge")

    def _ln_ep1(self, pss4, tmp):
        kb = self.kb
        for n in range(4):
            ps = pss4[n]
            sl = slice(n * 512, (n + 1) * 512)
            kb.op("act", lambda e, ps=ps, sl=sl: e.activation(out=tmp[:, sl], in_=ps[:], func=AF.Identity), reads=[ps], writes=[tmp])

    def _ln_ep2(self, G, xt, lng_bc, lnb_bc, tmp, stt, mv, rs, tm, extra_add=None):
        kb = self.kb
        if extra_add is not None:
            kb.op("dve", lambda e: e.tensor_tensor(out=tmp[:], in0=tmp[:], in1=extra_add[:], op=ALU.add), reads=[tmp, extra_add], writes=[tmp])
        kb.op("dve", lambda e: e.tensor_tensor(out=tmp[:], in0=tmp[:], in1=G[:], op=ALU.mult), reads=[tmp, G], writes=[tmp])
        kb.op("dve", lambda e: e.scalar_tensor_tensor(out=xt[:], in0=xt[:], scalar=ALPHA, in1=tmp[:], op0=ALU.mult, op1=ALU.add),
              reads=[xt, tmp], writes=[xt])
        for q in range(4):
            kb.op("dve", lambda e, q=q: e.bn_stats(out=stt[:, q, :], in_=xt[:, q * 512:(q + 1) * 512]), reads=[xt], writes=[stt])
        kb.op("dve", lambda e: e.bn_aggr(out=mv[:], in_=stt[:].rearrange("p a b -> p (a b)")), reads=[stt], writes=[mv])
        self._rstd(mv[:, 1:2], rs, tm, [mv])
        kb.op("dve", lambda e: e.tensor_scalar(out=xt[:], in0=xt[:], scalar1=mv[:, 0:1], scalar2=rs[:, 0:1], op0=ALU.subtract, op1=ALU.mult),
              reads=[xt, mv, rs], writes=[xt])
        kb.op("dve", lambda e: e.tensor_tensor(out=xt[:], in0=xt[:], in1=lng_bc[:], op=ALU.mult), reads=[xt, lng_bc], writes=[xt])
        kb.op("pool", lambda e: e.tensor_tensor(out=xt[:], in0=xt[:], in1=lnb_bc[:], op=ALU.add), reads=[xt, lnb_bc], writes=[xt])

    def _outproj(self, l, last):
        cfg, kb = self.cfg, self.kb
        T = cfg.T
        w_out = self.din("w_out", (self.wdep, D, D))
        ln1g = self.din("ln1_g", (self.wdep, D))
        ln1b = self.din("ln1_b", (self.wdep, D))
        mods = self._mods()
        xres = self._xres()
        mT = self.dscr("mT", (KC, 128, T), BF16)
        hfT = self.dscr("hfT", (KC, 128, T), BF16)
        with ExitStack() as st:
            idf, idb = self._ident(st)
            wo = self._sb(st, "wo", (128, KC, D), BF16)
            wol = w_out[self.li(l)].rearrange("(kc p) n -> p kc n", p=128)
            won = [wo] + [kb.buf(wo.t, dma=True, name="won%d" % g) for g in range(1, 4)]
            stg = Rot([self._sb(st, "stg%d" % i, (128, 8, 512), F32) for i in range(2)])
            for g in range(4):
                self._wload_hw(stg, wo[:, 0:8, g * 512:(g + 1) * 512], won[g], wol[:, 0:8, g * 512:(g + 1) * 512])
                self._wload_hw(stg, wo[:, 8:16, g * 512:(g + 1) * 512], won[g], wol[:, 8:16, g * 512:(g + 1) * 512])
            gm = self._sb(st, "gm", (128, D), F32)
            A2 = self._sb(st, "A2", (128, D), F32)
            B2 = self._sb(st, "B2", (128, D), F32)
            lg = self._sb(st, "lg", (128, D), F32)
            lb = self._sb(st, "lb", (128, D), F32)
            self._bcast_load(lg, ln1g[self.li(l), :])
            self._bcast_load(lb, ln1b[self.li(l), :])
            xt = Rot([self._sb(st, "xt%d" % i, (128, D), F32) for i in range(3)])
            tmpr = Rot([self._sb(st, "tmp%d" % i, (128, D), F32, dma=False) for i in range(2)])
            hb = Rot([self._sb(st, "hb%d" % i, (128, D), BF16, dma=False) for i in range(2)])
            mt = Rot([self._sb(st, "mt%d" % i, (128, KC, 128), BF16) for i in range(2)])
            hst = Rot([self._sb(st, "hst%d" % i, (128, KC, 128), BF16) for i in range(2)])
            sttr = Rot([self._sb(st, "stt%d" % i, (128, 4, 6), F32, dma=False) for i in range(2)])
            mvr = Rot([self._sb(st, "mv%d" % i, (128, 2), F32, dma=False) for i in range(2)])
            rsr = Rot([self._sb(st, "rs%d" % i, (128, 1), F32, dma=False) for i in range(2)])
            tmr = Rot([self._sb(st, "tm%d" % i, (128, 1), F32, dma=False) for i in range(2)])
            pss = [self._ps(st, "ps%d" % i) for i in range(4)]
            ptr = Rot([self._ps(st, "ptr%d" % i, (128, 4, 128), BF16) for i in range(4)])
            state = {"gm": -1, "ab": -1}
            ctxs = {}

            def mm(i):
                m_ = mt.next()
                kb.dma("act", m_[:], mT[:, :, i * 128:(i + 1) * 128].rearrange("c p t -> p c t"), writes=[m_])
                x = xt.next()
                kb.dma("act", x[:], xres[i * 128:(i + 1) * 128, :], writes=[x])
                for n in range(4):
                    kb.mm_group(pss[n], pss[n][:], [m_[:, k, :] for k in range(KC)], [wo[:, k, n * 512:(n + 1) * 512] for k in range(KC)], [m_, won[n]])
                ctxs[i] = {"x": x}

            def ep1(i):
                t_ = tmpr.next()
                ctxs[i]["tmp"] = t_
                self._ln_ep1(pss, t_)

            def ep2(i):
                sset = 0 if i < cfg.NTL else 1
                if sset != state["gm"]:
                    self._bcast_load(gm, mods[l * 2 + sset, 2 * D:3 * D])
                    state["gm"] = sset
                if sset != state["ab"]:
                    self._bcast_load(A2, mods[l * 2 + sset, 4 * D:5 * D])
                    self._bcast_load(B2, mods[l * 2 + sset, 3 * D:4 * D])
                    state["ab"] = sset
                x, t_ = ctxs[i]["x"], ctxs[i]["tmp"]
                self._ln_ep2(gm, x, lg, lb, t_, sttr.next(), mvr.next(), rsr.next(), tmr.next())
                kb.dma("sp", xres[i * 128:(i + 1) * 128, :], x[:], reads=[x])
                h_ = hb.next()
                kb.op("pool", lambda e: e.tensor_tensor(out=t_[:], in0=x[:], in1=A2[:], op=ALU.mult), reads=[x, A2], writes=[t_])
                kb.op("pool", lambda e: e.tensor_tensor(out=h_[:], in0=t_[:], in1=B2[:], op=ALU.add), reads=[t_, B2], writes=[h_])
                hs = hst.next()
                self._transpose_tile(h_, idb, ptr, hs, hs, 0, all_act=True)
                kb.dma("sp", hfT[:, :, i * 128:(i + 1) * 128].rearrange("c p t -> p c t"), hs[:], reads=[hs])
                del ctxs[i]

            n_t = self.tilesB
            mm(0)
            ep1(0)
            for i in range(1, n_t):
                mm(i)
                ep1(i)
                ep2(i - 1)
            ep2(n_t - 1)
            kb.run_stage("outproj")

    def _ffn_up(self, l):
        cfg, kb = self.cfg, self.kb
        T = cfg.T
        w_gu = self.din("w_gu", (self.wdep, D, 2 * FFN))
        hfT = self.dscr("hfT", (KC, 128, T), BF16)
        aT = self.dscr("aT", (cfg.NT, 128, FKC, 128), BF16)
        wl = w_gu[self.li(l)].rearrange("(kc p) n -> p kc n", p=128)
        Tb = self.tilesB * 128
        with ExitStack() as st:
            hf = self._sb(st, "hf", (128, KC, T), BF16)
            for g in range(4):
                kb.dma("sp", hf[:, g * 4:(g + 1) * 4, 0:Tb], hfT[g * 4:(g + 1) * 4, :, 0:Tb].rearrange("c p t -> p c t"), writes=[hf])
            wga = Rot([self._sb(st, "wga%d" % i, (128, KC, 512), BF16) for i in range(2)])
            wua = Rot([self._sb(st, "wua%d" % i, (128, KC, 512), BF16) for i in range(2)])
            stg = Rot([self._sb(st, "stg%d" % i, (128, 8, 512), F32) for i in range(3)])
            pss = Rot([self._ps(st, "ps%d" % i) for i in range(8)])
            sgb = Rot([self._sb(st, "sgb%d" % i, (128, 512), F32, dma=False) for i in range(3)])
            ab = Rot([self._sb(st, "ab%d" % i, (128, 512), BF16) for i in range(3)])
            tasks = []
            for g in range(FKC // 4):
                hold = {}

                def ld(g=g, hold=hold):
                    hold["g"], hold["u"] = wga.next(), wua.next()
                    self._wload2(stg, hold["g"], wl[:, :, g * 512:(g + 1) * 512])
                    self._wload2(stg, hold["u"], wl[:, :, FFN + g * 512:FFN + (g + 1) * 512])

                def cp(g=g, hold=hold):
                    wg_, wu_ = hold["g"], hold["u"]
                    for jj in range(4):
                        j = g * 4 + jj
                        for blk in self.blocksB:
                            t0, bs = blk
                            pg, pu = pss.next(), pss.next()
                            kb.mm_group(pg, pg[:, 0:bs], [wg_[:, k, jj * 128:(jj + 1) * 128] for k in range(KC)], [hf[:, k, t0:t0 + bs] for k in range(KC)], [wg_, hf])
                            kb.mm_group(pu, pu[:, 0:bs], [wu_[:, k, jj * 128:(jj + 1) * 128] for k in range(KC)], [hf[:, k, t0:t0 + bs] for k in range(KC)], [wu_, hf])
                            s_ = sgb.next()
                            kb.op("act", lambda e, s_=s_, pg=pg, bs=bs: e.activation(out=s_[:, 0:bs], in_=pg[:, 0:bs], func=AF.Silu), reads=[pg], writes=[s_])
                            a_ = ab.next()
                            kb.op("dve", lambda e, a_=a_, s_=s_, pu=pu, bs=bs: e.tensor_tensor(out=a_[:, 0:bs], in0=pu[:, 0:bs], in1=s_[:, 0:bs], op=ALU.mult),
                                  reads=[pu, s_], writes=[a_])
                            i0 = t0 // 128
                            nt = bs // 128
                            kb.dma("sp", aT[i0:i0 + nt, :, j, :].rearrange("i p t -> p i t"), a_[:, 0:bs].rearrange("p (i t) -> p i t", t=128), reads=[a_])
                tasks.append((ld, cp))
            self._pipeline(tasks)
            kb.run_stage("ffn_up")

    def _ffn_down(self, l, last):
        cfg, kb = self.cfg, self.kb
        T = cfg.T
        w_down = self.din("w_down", (self.wdep, FFN, D))
        ln2g = self.din("ln2_g", (self.wdep, D))
        ln2b = self.din("ln2_b", (self.wdep, D))
        mods = self._mods()
        xres = self._xres()
        aT = self.dscr("aT", (cfg.NT, 128, FKC, 128), BF16)
        part = self.dscr("part", (T, D), F32)
        outp = self.dout("out", (cfg.LAT, D)) if last else None
        wdl = w_down[self.li(l)].rearrange("(kc p) n -> p kc n", p=128)
        H = FKC // 2
        for ps_i in range(2):
            with ExitStack() as st:
                wd = self._sb(st, "wd", (128, H, D), BF16)
                wdk = [wd] + [kb.buf(wd.t, dma=True, name="wdk%d" % k) for k in range(1, H)]
                stg = Rot([self._sb(st, "stg%d" % i, (128, D), F32) for i in range(2)])
                for k in range(H):
                    self._wload_hw(stg, wd[:, k, :], wdk[k], wdl[:, ps_i * H + k, :])
                at = Rot([self._sb(st, "at%d" % i, (128, H, 128), BF16) for i in range(2)])
                pss = [self._ps(st, "ps%d" % i) for i in range(8)]
                if ps_i == 0:
                    pt = Rot([self._sb(st, "pt%d" % i, (128, D), F32) for i in range(2)])
                else:
                    gf = self._sb(st, "gf", (128, D), F32)
                    lg = self._sb(st, "lg", (128, D), F32)
                    lb = self._sb(st, "lb", (128, D), F32)
                    self._bcast_load(lg, ln2g[self.li(l), :])
                    self._bcast_load(lb, ln2b[self.li(l), :])
                    xt = Rot([self._sb(st, "xt%d" % i, (128, D), F32) for i in range(2)])
                    pt = Rot([self._sb(st, "pt%d" % i, (128, D), F32) for i in range(2)])
                    tmpr = Rot([self._sb(st, "tmp%d" % i, (128, D), F32, dma=False) for i in range(2)])
                    sttr = Rot([self._sb(st, "stt%d" % i, (128, 4, 6), F32, dma=False) for i in range(2)])
                    mvr = Rot([self._sb(st, "mv%d" % i, (128, 2), F32, dma=False) for i in range(2)])
                    rsr = Rot([self._sb(st, "rs%d" % i, (128, 1), F32, dma=False) for i in range(2)])
                    tmr = Rot([self._sb(st, "tm%d" % i, (128, 1), F32, dma=False) for i in range(2)])
                cur = {"set": -1}
                lds = {}

                def loads(i, ps_i=ps_i):
                    a_ = at.next()
                    kb.dma("sp", a_[:], aT[i, :, ps_i * H:(ps_i + 1) * H, :], writes=[a_])
                    d_ = {"a": a_}
                    if ps_i == 1:
                        p_ = pt.next()
                        kb.dma("sp", p_[:], part[i * 128:(i + 1) * 128, :], writes=[p_])
                        x = xt.next()
                        kb.dma("sp", x[:], xres[i * 128:(i + 1) * 128, :], writes=[x])
                        d_["p"], d_["x"] = p_, x
                    lds[i] = d_

                def compute(i, ps_i=ps_i):
                    d_ = lds.pop(i)
                    a_ = d_["a"]
                    p4 = pss[(i % 2) * 4:(i % 2) * 4 + 4]
                    for n in range(4):
                        kb.mm_group(p4[n], p4[n][:], [a_[:, k, :] for k in range(H)], [wd[:, k, n * 512:(n + 1) * 512] for k in range(H)], [a_], reads_k=wdk)
                    if ps_i == 0:
                        p_ = pt.next()
                        for n in range(4):
                            if n % 2 == 0:
                                kb.op("act", lambda e, p_=p_, n=n, p4=p4: e.activation(out=p_[:, n * 512:(n + 1) * 512], in_=p4[n][:], func=AF.Identity),
                                      reads=[p4[n]], writes=[p_])
                            else:
                                kb.op("dve", lambda e, p_=p_, n=n, p4=p4: e.tensor_copy(out=p_[:, n * 512:(n + 1) * 512], in_=p4[n][:]),
                                      reads=[p4[n]], writes=[p_])
                        kb.dma("act", part[i * 128:(i + 1) * 128, :], p_[:], reads=[p_])
                    else:
                        sset = 0 if i < cfg.NTL else 1
                        if sset != cur["set"]:
                            self._bcast_load(gf, mods[l * 2 + sset, 5 * D:6 * D])
                            cur["set"] = sset
                        p_, x = d_["p"], d_["x"]
                        t_ = tmpr.next()
                        self._ln_ep1(p4, t_)
                        self._ln_ep2(gf, x, lg, lb, t_, sttr.next(), mvr.next(), rsr.next(), tmr.next(), extra_add=p_)
                        if last:
                            kb.dma("act", outp[i * 128:(i + 1) * 128, :], x[:], reads=[x])
                        else:
                            kb.dma("act", xres[i * 128:(i + 1) * 128, :], x[:], reads=[x])

                n_t = self.tilesB
                loads(0)
                for i in range(n_t):
                    if i + 1 < n_t:
                        loads(i + 1)
                    compute(i)
                kb.run_stage("ffn_down%d" % ps_i)


def _rope_tables(pos0, lat, grid_w=64):
    p = np.arange(128)
    dd = p % 64
    axis = dd // 32
    a = dd % 32
    half = a // 16
    pair = a % 16
    inv = (10000.0 ** (-np.arange(16, dtype=np.float32) / 16)).astype(np.float32)
    t = pos0 + np.arange(lat)
    row = (t // grid_w).astype(np.float32)
    col = (t % grid_w).astype(np.float32)
    posax = np.where(axis[:, None] == 0, row[None, :], col[None, :]).astype(np.float32)
    ang = (posax * inv[pair][:, None]).astype(np.float32)
    cos = np.cos(ang).astype(np.float32)
    sin = np.sin(ang).astype(np.float32)
    sgn = np.where(half == 0, -1.0, 1.0).astype(np.float32)[:, None]
    c = np.ones((128, lat + 128), np.float32)
    s = np.zeros((128, lat + 128), np.float32)
    c[:, :lat] = cos
    s[:, :lat] = sin * sgn
    return c, s


def _rope_perm():
    p = np.arange(128)
    a = (p % 64) % 32
    half = a // 16
    partner = np.where(half == 0, p + 16, p - 16)
    return partner


def _pcorr(half, lat, nseq_lat, nseq_ctx):
    out = np.ones((2, 4, 16), np.float32)
    for si, (nloc, n) in enumerate(((lat, nseq_lat), (128, nseq_ctx))):
        g0 = half * nloc
        for wi, w in enumerate(WINS):
            for j in range(16):
                tl = j if j < 8 else nloc - 16 + j
                t = g0 + tl
                lo = min(max(t - w // 2, 0), n)
                hi = min(max(t - w // 2 + w, 0), n)
                out[si, wi, j] = w / float(hi - lo)
    return out


_PROG_CACHE = {}


def _get_prog(key, cfg, phases, mods_ext, xchg_ext, fused=False):
    if key not in _PROG_CACHE:
        import time as _t
        t0 = _t.time()
        p = Prog(cfg, phases)
        p.mods_ext = mods_ext
        p.xchg_ext = xchg_ext
        p.fused = fused
        p.wdep = cfg.depth if fused else 1
        p.build()
        if VERBOSE:
            print("build %s: %.1fs" % (str(key), _t.time() - t0), flush=True)
        _PROG_CACHE[key] = p
    return _PROG_CACHE[key]


def _run(prog, per_core):
    in_maps = []
    for c in range(8):
        in_maps.append({n: per_core[c][n] for n in prog.inputs})
    import time as _t
    t0 = _t.time()
    res = run_bass_kernel_spmd(prog.nc, in_maps, core_ids=list(range(8)))
    if VERBOSE:
        nb = sum(v.nbytes for v in in_maps[0].values())
        print("launch: %.1fs, in bytes/core %.1f MB" % (_t.time() - t0, nb / 1e6), flush=True)
    return res.results


def kernel(x, c, ctx, c_ctx, w_ada, b_ada, w_in, lam_qk, subln_g, gmlp_ln_g, gmlp_ln_b,
           w_spatial, b_spatial, w_pool, pool_scale, w_branch, w_out, ln1_g, ln1_b,
           w_gu, w_down, ln2_g, ln2_b):
    f = lambda a: np.ascontiguousarray(np.asarray(a), dtype=np.float32)
    x, c, ctx, c_ctx = f(x), f(c), f(ctx), f(c_ctx)
    B, S, _ = x.shape
    depth = w_in.shape[0]
    lat = S // 2
    cfg = Cfg(lat, depth)
    T = cfg.T
    w_in = f(w_in)
    perm = _rope_perm()
    colperm = np.concatenate([h * 128 + perm for h in range(16)])
    shared = {
        "ident": np.eye(128, dtype=np.float32),
        "w_ada": f(w_ada), "b_ada": f(b_ada), "w_in": w_in,
        "w_qkp": np.ascontiguousarray(w_in[:, :, :2048][:, :, colperm]),
        "lam_qk": f(lam_qk).reshape(depth, 256),
        "subln_g": f(subln_g).reshape(depth, 128, 1),
        "gmlp_ln_g": f(gmlp_ln_g), "gmlp_ln_b": f(gmlp_ln_b),
        "wsT": np.ascontiguousarray(f(w_spatial).transpose(0, 3, 1, 2)),
        "b_spatial": f(b_spatial).reshape(depth, 1024),
        "w_pool": f(w_pool),
        "pool_scaleT": np.ascontiguousarray(f(pool_scale).reshape(depth, 8, 128).transpose(0, 2, 1)),
        "w_branch": f(w_branch), "w_out": f(w_out), "ln1_g": f(ln1_g), "ln1_b": f(ln1_b),
        "w_gu": f(w_gu), "w_down": f(w_down), "ln2_g": f(ln2_g), "ln2_b": f(ln2_b),
    }
    per_core = []
    for core in range(8):
        b, half = core // 2, core % 2
        d = dict(shared)
        d["x_raw"] = np.concatenate([x[b, half * lat:(half + 1) * lat], ctx[b, half * 128:(half + 1) * 128]], axis=0)
        cc = np.stack([c[b], c_ctx], axis=1)
        d["cT"] = np.ascontiguousarray(cc.reshape(KC, 128, 2).transpose(1, 0, 2))
        rc, rs_ = _rope_tables(half * lat, lat)
        d["rope_cos"], d["rope_sin"] = rc, rs_
        hm = np.zeros((128, 2), np.float32)
        hm[:, 0] = 1.0 if half == 1 else 0.0
        hm[:, 1] = 1.0 if half == 0 else 0.0
        d["hmask"] = hm
        d["pcorr"] = np.ascontiguousarray(np.broadcast_to(_pcorr(half, lat, S, 256)[None], (128, 2, 4, 16)))
        per_core.append(d)

    LAYER_W = ["w_in", "w_qkp", "lam_qk", "subln_g", "wsT", "b_spatial", "gmlp_ln_g", "gmlp_ln_b", "w_pool",
               "pool_scaleT", "w_branch", "w_out", "ln1_g", "ln1_b", "w_gu", "w_down", "ln2_g", "ln2_b"]

    def layer_inputs(l):
        for core in range(8):
            for n in LAYER_W:
                per_core[core][n] = shared[n][l:l + 1]

    if FUSED:
        phases = [("norm0", 0), ("mods", 0)]
        for l in range(depth):
            phases += [("A", l), ("xchg", l), ("B", l)]
        pf = _get_prog(("F", lat, depth), cfg, phases, False, False, fused=True)
        rf = _run(pf, per_core)
        out = np.zeros((B, S, D), np.float32)
        for core in range(8):
            b, half = core // 2, core % 2
            out[b, half * lat:(half + 1) * lat] = rf[core]["out"]
        return out

    p0 = _get_prog(("L0", lat, depth), cfg, [("norm0", 0), ("mods", 0), ("xout", 0)], True, True)
    r0 = _run(p0, per_core)
    for core in range(8):
        per_core[core]["x_in"] = r0[core]["x_out"]
        per_core[core]["mods"] = r0[core]["mods"]
    out = None
    for l in range(depth):
        last = l == depth - 1
        layer_inputs(l)
        pa = _get_prog(("A", lat, depth, l), cfg, [("xin", l), ("A", l)], True, True)
        ra = _run(pa, per_core)
        for core in range(8):
            pr = [2 * (core // 2), 2 * (core // 2) + 1]
            for h in range(8):
                per_core[core]["KT_all%d" % h] = np.concatenate([ra[q]["KT_x%d" % h] for q in pr], axis=0)
                per_core[core]["V_all%d" % h] = np.concatenate([ra[q]["V_x%d" % h] for q in pr], axis=0)
            per_core[core]["halo_all"] = np.concatenate([ra[q]["halo_x"] for q in pr], axis=0)
        phases = [("xin", l), ("B", l)] + ([] if last else [("xout", l)])
        pb = _get_prog(("B", lat, depth, l), cfg, phases, True, True)
        rb = _run(pb, per_core)
        if DEBUG is not None:
            DEBUG.append((ra, rb))
        if last:
            out = np.zeros((B, S, D), np.float32)
            for core in range(8):
                b, half = core // 2, core % 2
                out[b, half * lat:(half + 1) * lat] = rb[core]["out"]
        else:
            for core in range(8):
                per_core[core]["x_in"] = rb[core]["x_out"]
    return out


DEBUG = None
FUSED = True
VERBOSE = False
```

```python
import math
import numpy as np
import ml_dtypes
from contextlib import ExitStack
import concourse.bass as bass
import concourse.mybir as mybir
from concourse.bass_utils import run_bass_kernel_spmd

F32 = mybir.dt.float32
BF16 = mybir.dt.bfloat16
F32R = mybir.dt.float32r
AF = mybir.ActivationFunctionType
ALU = mybir.AluOpType

ENGS = ("pe", "act", "dve", "pool", "sp")
N_DMA_SEMS = 60

D = 2048
KC = 16
DEPTH = 4
FFN = 5632
FKC = 44
IN_W = 12288
K_OFF, V_OFF, BU_OFF, BV_OFF, C_OFF, G_OFF = 1024, 2048, 3072, 4096, 5120, 6144
ALPHA = (2 * DEPTH) ** 0.25
EPS = 1e-6
A_SCALE = 0.125
WINS = (2, 4, 8, 16)


class Buf:
    def __init__(self, t=None, dsem=None, name=""):
        self.t = t
        self.dsem = dsem
        self.w = None
        self.r = []
        self.name = name

    def __getitem__(self, k):
        return self.t[k]


class Rot:
    def __init__(self, bufs):
        self.bufs = bufs
        self.i = 0

    def next(self):
        b = self.bufs[self.i % len(self.bufs)]
        self.i += 1
        return b


class KB:
    def __init__(self, nc, stack):
        self.nc = nc
        self.sem = {}
        self.cnt = {}
        for e in ENGS:
            self.sem[e] = stack.enter_context(nc.semaphore("s_" + e))
            self.cnt[e] = 0
        self.dma_sems = []
        for i in range(N_DMA_SEMS):
            n = "d%d" % i
            self.sem[n] = stack.enter_context(nc.semaphore("s_" + n))
            self.cnt[n] = 0
            self.dma_sems.append(n)
        self.seen = {e: {} for e in ENGS}
        self.ops = {e: [] for e in ENGS}
        self.next_dsem = 0
        self.stage_bufs = []
        self.stage_dsems = set()
        self.nstage = 0

    def buf(self, t=None, dma=True, name=""):
        ds = None
        if dma:
            ds = self.dma_sems[self.next_dsem % N_DMA_SEMS]
            self.next_dsem += 1
        b = Buf(t, ds, name)
        self.stage_bufs.append(b)
        return b

    def _waits(self, e, reads, writes):
        need = {}

        def add(ev):
            if ev is None:
                return
            s, c = ev
            if e == "pe" and s == "pe":
                return
            if self.seen[e].get(s, 0) < c:
                need[s] = max(need.get(s, 0), c)

        for b in reads:
            add(b.w)
        for b in writes:
            add(b.w)
            for ev in b.r:
                add(ev)
        for s, c in need.items():
            self.seen[e][s] = c
            h = self.sem[s]
            self.ops[e].append(lambda eng, h=h, c=c: eng.wait_ge(h, c))

    def _mark(self, ev, reads, writes):
        for b in reads:
            if b not in writes:
                b.r.append(ev)
        for b in writes:
            b.w = ev
            b.r = []

    def op(self, e, fn, reads=(), writes=(), inc=True):
        self._waits(e, reads, writes)
        h = self.sem[e]
        if inc:
            self.cnt[e] += 1
            ev = (e, self.cnt[e])
            self.ops[e].append(lambda eng, fn=fn, h=h: fn(eng).then_inc(h, 1))
        else:
            ev = (e, self.cnt[e] + 1)
            self.ops[e].append(lambda eng, fn=fn: fn(eng))
        self._mark(ev, reads, writes)

    def dma(self, e, out, in_, reads=(), writes=(), dsem=None, **kw):
        if dsem is None:
            for b in list(writes) + list(reads):
                if b.dsem is not None:
                    dsem = b.dsem
                    break
        assert dsem is not None
        self._waits(e, reads, writes)
        self.cnt[dsem] += 16
        ev = (dsem, self.cnt[dsem])
        h = self.sem[dsem]
        self.stage_dsems.add(dsem)
        self.ops[e].append(lambda eng, out=out, in_=in_, h=h, kw=kw:
                           eng.dma_start(out=out, in_=in_, **kw).then_inc(h, 16))
        self._mark(ev, reads, writes)

    def mm_group(self, ps, ps_ap, lhs_list, rhs_list, reads, reads_k=None):
        n = len(lhs_list)
        for k in range(n):
            rd = list(reads) + ([reads_k[k]] if reads_k is not None else [])
            self.op("pe", lambda e, k=k: e.matmul(ps_ap, lhsT=lhs_list[k], rhs=rhs_list[k],
                                                  start=(k == 0), stop=(k == n - 1)),
                    reads=rd, writes=[ps], inc=(k == n - 1))

    def run_stage(self, name=None):
        nc = self.nc
        self.nstage += 1
        name = "%s_%d" % (name or "st", self.nstage)
        for s in sorted(self.stage_dsems):
            c = self.cnt[s]
            if self.seen["sp"].get(s, 0) < c:
                h = self.sem[s]
                self.ops["sp"].append(lambda eng, h=h, c=c: eng.wait_ge(h, c))
        for e in ("pe", "act", "dve", "pool"):
            c = self.cnt[e]
            if c > 0 and self.seen["sp"].get(e, 0) < c:
                h = self.sem[e]
                self.ops["sp"].append(lambda eng, h=h, c=c: eng.wait_ge(h, c))
        ops = self.ops
        with nc.Block(name) as block:
            @block.sync
            def _(eng):
                for f in ops["sp"]:
                    f(eng)

            @block.tensor
            def _(eng):
                for f in ops["pe"]:
                    f(eng)

            @block.scalar
            def _(eng):
                for f in ops["act"]:
                    f(eng)

            @block.vector
            def _(eng):
                for f in ops["dve"]:
                    f(eng)

            @block.gpsimd
            def _(eng):
                for f in ops["pool"]:
                    f(eng)
        for e in ENGS:
            for s in self.cnt:
                self.seen[e][s] = self.cnt[s]
        self.ops = {e: [] for e in ENGS}
        for b in self.stage_bufs:
            b.w = None
            b.r = []
        self.stage_bufs = []
        self.stage_dsems = set()


class Cfg:
    def __init__(self, lat, depth=DEPTH):
        self.LAT = lat
        self.CTX = 128
        self.T = lat + 128
        self.NT = self.T // 128
        self.NTL = lat // 128
        self.depth = depth
        self.blocks = []
        t = 0
        while t < lat:
            bs = min(512, lat - t)
            self.blocks.append((t, bs))
            t += bs
        self.lat_blocks = list(self.blocks)
        self.blocks.append((lat, 128))


class Prog:
    def __init__(self, cfg, phases):
        self.cfg = cfg
        self.nc = bass.Bass("TRN2", target_bir_lowering=False)
        self.phases = phases
        self.inputs = []
        self.outputs = []
        self.dr = {}
        self.fused = False
        self.wdep = 1
        self.drt = {}
        self.ccs = None
        self.ccnt = 0

    def li(self, l):
        return l if self.fused else 0

    def din(self, name, shape, dt=F32):
        if name not in self.dr:
            self.dr[name] = self.nc.dram_tensor(name, list(shape), dt, kind="ExternalInput").ap()
            self.inputs.append(name)
        return self.dr[name]

    def dout(self, name, shape, dt=F32):
        if name not in self.dr:
            self.dr[name] = self.nc.dram_tensor(name, list(shape), dt, kind="ExternalOutput").ap()
            self.outputs.append(name)
        return self.dr[name]

    def dcc(self, name, shape, dt=F32):
        if name not in self.dr:
            t = self.nc.dram_tensor(name, list(shape), dt)
            self.drt[name] = t
            self.dr[name] = t.ap()
        return self.dr[name]

    def ph_xchg(self, l):
        kb = self.kb
        cfg = self.cfg
        T = cfg.T
        for h in range(8):
            self.dcc("KT_x%d" % h, (128, T), BF16)
            self.dcc("V_x%d" % h, (T, 128), BF16)
        self.dcc("halo_x", (8 * 128, 32), F32)
        self._gathered()
        if self.ccs is None:
            self.ccs = self._stack.enter_context(self.nc.semaphore("cc_sem"))
        ccs = self.ccs
        pairs = [[0, 1], [2, 3], [4, 5], [6, 7]]
        names = [("KT_x%d" % h, "KT_all%d" % h) for h in range(8)] + [("V_x%d" % h, "V_all%d" % h) for h in range(8)] + [("halo_x", "halo_all")]
        for a, b in names:
            ta, tb = self.drt[a], self.drt[b]
            kb.ops["pool"].append(lambda eng, ta=ta, tb=tb: eng.collective_compute(
                "AllGather", ALU.bypass, replica_groups=pairs, ins=[ta.ap().opt()], outs=[tb.ap().opt()]).then_inc(ccs))
            self.ccnt += 1
        c = self.ccnt
        kb.ops["pool"].append(lambda eng, c=c: eng.wait_ge(ccs, c))
        kb.run_stage("xchg")

    def dscr(self, name, shape, dt=F32):
        if name not in self.dr:
            self.dr[name] = self.nc.dram_tensor(name, list(shape), dt, kind="Internal").ap()
        return self.dr[name]

    def build(self):
        nc = self.nc
        with ExitStack() as st:
            self._stack = st
            self.kb = KB(nc, st)
            for ph, l in self.phases:
                getattr(self, "ph_" + ph)(l)
        return nc

    def _un(self, name):
        self._uid = getattr(self, "_uid", 0) + 1
        return "%s_%d" % (name, self._uid)

    def _sb(self, st, name, shape, dt, dma=True):
        name = self._un(name)
        t = st.enter_context(self.nc.sbuf_tensor(name, list(shape), dt))
        return self.kb.buf(t, dma=dma, name=name)

    def _ps(self, st, name, shape=(128, 512), dt=F32):
        name = self._un(name)
        t = st.enter_context(self.nc.psum_tensor(name, list(shape), dt))
        return self.kb.buf(t, dma=False, name=name)

    def _ident(self, st):
        kb = self.kb
        idin = self.din("ident", (128, 128))
        idf = self._sb(st, "idf", (128, 128), F32)
        idb = self._sb(st, "idb", (128, 128), BF16)
        kb.dma("sp", idf[:], idin, writes=[idf])
        kb.op("dve", lambda e: e.tensor_copy(out=idb[:], in_=idf[:]), reads=[idf], writes=[idb])
        return idf, idb

    def _rstd(self, var_ap, out_buf, tmp_buf, reads):
        kb = self.kb
        kb.op("dve", lambda e: e.tensor_scalar(out=tmp_buf[:], in0=var_ap, scalar1=EPS, scalar2=None, op0=ALU.add),
              reads=reads, writes=[tmp_buf])
        kb.op("act", lambda e: e.activation(out=tmp_buf[:], in_=tmp_buf[:], func=AF.Sqrt), reads=[tmp_buf], writes=[tmp_buf])
        kb.op("dve", lambda e: e.reciprocal(out=out_buf[:], in_=tmp_buf[:]), reads=[tmp_buf], writes=[out_buf])

    def _bcast_load(self, buf, row_ap):
        self.kb.dma("sp", buf[:], row_ap.partition_broadcast(128), writes=[buf])

    def _mods(self):
        return (self.din if self.mods_ext else self.dscr)("mods", (self.cfg.depth * 2, 6 * D))

    def _xres(self):
        return self.dscr("xres", (self.cfg.T, D))

    def ph_xin(self, l):
        cfg, kb = self.cfg, self.kb
        xin = self.din("x_in", (cfg.T, D))
        xres = self._xres()
        with ExitStack() as st:
            tb = [self._sb(st, "cp%d" % i, (128, D), F32) for i in range(2)]
            for i in range(cfg.NT):
                b = tb[i % 2]
                kb.dma("sp", b[:], xin[i * 128:(i + 1) * 128, :], writes=[b])
                kb.dma("sp", xres[i * 128:(i + 1) * 128, :], b[:], reads=[b])
            kb.run_stage("xin")

    def ph_xout(self, l):
        cfg, kb = self.cfg, self.kb
        xo = self.dout("x_out", (cfg.T, D))
        xres = self._xres()
        with ExitStack() as st:
            tb = [self._sb(st, "cp%d" % i, (128, D), F32) for i in range(2)]
            for i in range(cfg.NT):
                b = tb[i % 2]
                kb.dma("sp", b[:], xres[i * 128:(i + 1) * 128, :], writes=[b])
                kb.dma("sp", xo[i * 128:(i + 1) * 128, :], b[:], reads=[b])
            kb.run_stage("xout")

    def ph_norm0(self, l):
        cfg, kb = self.cfg, self.kb
        xin = self.din("x_raw", (cfg.T, D))
        xres = self._xres()
        with ExitStack() as st:
            xt = [self._sb(st, "xt%d" % i, (128, D), F32) for i in range(2)]
            stt = [self._sb(st, "stt%d" % i, (128, 4, 6), F32, dma=False) for i in range(2)]
            mv = [self._sb(st, "mv%d" % i, (128, 2), F32, dma=False) for i in range(2)]
            rs = [self._sb(st, "rs%d" % i, (128, 1), F32, dma=False) for i in range(2)]
            tm = [self._sb(st, "tm%d" % i, (128, 1), F32, dma=False) for i in range(2)]
            for i in range(cfg.NT):
                x, s_, m_, r_, t_ = xt[i % 2], stt[i % 2], mv[i % 2], rs[i % 2], tm[i % 2]
                kb.dma("sp", x[:], xin[i * 128:(i + 1) * 128, :], writes=[x])
                for q in range(4):
                    kb.op("dve", lambda e, q=q, x=x, s_=s_: e.bn_stats(out=s_[:, q, :], in_=x[:, q * 512:(q + 1) * 512]),
                          reads=[x], writes=[s_])
                kb.op("dve", lambda e, s_=s_, m_=m_: e.bn_aggr(out=m_[:], in_=s_[:].rearrange("p a b -> p (a b)")),
                      reads=[s_], writes=[m_])
                self._rstd(m_[:, 1:2], r_, t_, [m_])
                kb.op("dve", lambda e, x=x, m_=m_, r_=r_: e.tensor_scalar(out=x[:], in0=x[:], scalar1=m_[:, 0:1], scalar2=r_[:, 0:1],
                                                                          op0=ALU.subtract, op1=ALU.mult),
                      reads=[x, m_, r_], writes=[x])
                kb.dma("sp", xres[i * 128:(i + 1) * 128, :], x[:], reads=[x])
            kb.run_stage("norm0")

    def ph_mods(self, l):
        cfg, kb = self.cfg, self.kb
        cT = self.din("cT", (128, KC, 2))
        w_ada = self.din("w_ada", (cfg.depth, D, 6 * D))
        b_ada = self.din("b_ada", (cfg.depth, 6 * D))
        mods = self.dout("mods", (cfg.depth * 2, 6 * D)) if self.mods_ext else self.dscr("mods", (cfg.depth * 2, 6 * D))
        with ExitStack() as st:
            sc = self._sb(st, "sc", (128, KC, 2), F32)
            kb.dma("sp", sc[:], cT, writes=[sc])
            kb.op("act", lambda e: e.activation(out=sc[:], in_=sc[:], func=AF.Silu), reads=[sc], writes=[sc])
            scb = self._sb(st, "scb", (128, KC, 2), BF16, dma=False)
            kb.op("dve", lambda e: e.tensor_copy(out=scb[:], in_=sc[:]), reads=[sc], writes=[scb])
            wb = Rot([self._sb(st, "wa%d" % i, (128, 8, 512), F32) for i in range(4)])
            wq = Rot([self._sb(st, "wq%d" % i, (128, KC, 512), BF16, dma=False) for i in range(3)])
            bb = Rot([self._sb(st, "ba%d" % i, (2, 512), F32) for i in range(2)])
            ob = Rot([self._sb(st, "oa%d" % i, (2, 512), F32) for i in range(2)])
            pss = Rot([self._ps(st, "pm%d" % i, (2, 512)) for i in range(2)])
            cnt = 0
            for ll in range(cfg.depth):
                wl = w_ada[ll].rearrange("(kc p) n -> p kc n", p=128)
                for j in range(24):
                    wqt = wq.next()
                    for hh in range(2):
                        w = wb.next()
                        cnt += 1
                        kb.dma("sp" if cnt % 2 else "act", w[:], wl[:, hh * 8:(hh + 1) * 8, j * 512:(j + 1) * 512], writes=[w])
                        if cnt % 2:
                            kb.op("dve", lambda e, w=w, wqt=wqt, hh=hh: e.tensor_copy(out=wqt[:, hh * 8:(hh + 1) * 8, :], in_=w[:]), reads=[w], writes=[wqt])
                        else:
                            kb.op("act", lambda e, w=w, wqt=wqt, hh=hh: e.activation(out=wqt[:, hh * 8:(hh + 1) * 8, :], in_=w[:], func=AF.Identity), reads=[w], writes=[wqt])
                    b_ = bb.next()
                    kb.dma("sp", b_[:], b_ada[ll, j * 512:(j + 1) * 512].partition_broadcast(2), writes=[b_])
                    ps = pss.next()
                    kb.mm_group(ps, ps[:], [scb[:, k, :] for k in range(KC)], [wqt[:, k, :] for k in range(KC)], [scb, wqt])
                    o = ob.next()
                    kb.op("dve", lambda e, o=o, ps=ps, b_=b_: e.tensor_tensor(out=o[:], in0=ps[:], in1=b_[:], op=ALU.add),
                          reads=[ps, b_], writes=[o])
                    if j // 4 in (1, 4):
                        kb.op("dve", lambda e, o=o: e.tensor_scalar(out=o[:], in0=o[:], scalar1=1.0, scalar2=None, op0=ALU.add),
                              reads=[o], writes=[o])
                    kb.dma("sp", mods[ll * 2:ll * 2 + 2, j * 512:(j + 1) * 512], o[:], reads=[o])
            kb.run_stage("mods")

    def _make_hT(self, st, l, chunk_sh, chunk_sc, name):
        cfg, kb = self.cfg, self.kb
        xres = self._xres()
        mods = self._mods()
        if self.fused and getattr(self, "_hT_saved", None) is not None and self._hT_saved[0] == l:
            _, hT, hTb = self._hT_saved
            self._hT_saved = None
            return hT, hTb
        hT = st.enter_context(self.nc.sbuf_tensor(self._un("hT"), [128, KC, cfg.T], BF16))
        hTb = [kb.buf(hT, dma=True, name="hT%d" % i) for i in range(cfg.NT)]
        with ExitStack() as s2:
            idf, idb = self._ident(s2)
            A = [self._sb(s2, "mA%d" % s, (128, D), F32) for s in range(2)]
            Bv = [self._sb(s2, "mB%d" % s, (128, D), F32) for s in range(2)]
            for s in range(2):
                r = l * 2 + s
                self._bcast_load(A[s], mods[r, chunk_sc * D:(chunk_sc + 1) * D])
                self._bcast_load(Bv[s], mods[r, chunk_sh * D:(chunk_sh + 1) * D])
            xt = Rot([self._sb(s2, "hx%d" % i, (128, D), F32) for i in range(3)])
            tmp = Rot([self._sb(s2, "ht%d" % i, (128, D), F32, dma=False) for i in range(2)])
            hb = Rot([self._sb(s2, "hb%d" % i, (128, D), BF16, dma=False) for i in range(2)])
            ptr = Rot([self._ps(s2, "ptr%d" % i, (128, 4, 128), BF16) for i in range(4)])
            for i in range(cfg.NT):
                s = 0 if i < cfg.NTL else 1
                x, t_, h_ = xt.next(), tmp.next(), hb.next()
                kb.dma("sp" if i % 2 == 0 else "act", x[:], xres[i * 128:(i + 1) * 128, :], writes=[x])
                kb.op("dve", lambda e, x=x, t_=t_, s=s: e.tensor_tensor(out=t_[:], in0=x[:], in1=A[s][:], op=ALU.mult),
                      reads=[x, A[s]], writes=[t_])
                if i % 3 == 2:
                    kb.op("pool", lambda e, h_=h_, t_=t_, s=s: e.tensor_tensor(out=h_[:], in0=t_[:], in1=Bv[s][:], op=ALU.add),
                          reads=[t_, Bv[s]], writes=[h_])
                else:
                    kb.op("dve", lambda e, h_=h_, t_=t_, s=s: e.tensor_tensor(out=h_[:], in0=t_[:], in1=Bv[s][:], op=ALU.add),
                          reads=[t_, Bv[s]], writes=[h_])
                self._transpose_tile(h_, idb, ptr, hT, hTb[i], i)
            kb.run_stage(name)
        return hT, hTb

    def _transpose_tile(self, h_, idb, ptr, hT, hTbuf, i, all_act=False):
        kb = self.kb
        for q in range(4):
            pt = ptr.next()
            for r in range(4):
                kc = q * 4 + r
                kb.op("pe", lambda e, pt=pt, r=r, kc=kc, h_=h_: e.transpose(out=pt[:, r, :], in_=h_[:, kc * 128:(kc + 1) * 128],
                                                                         identity=idb[:]),
                      reads=[h_, idb], writes=[pt], inc=(r == 3))
            oap = hT[:, q * 4:(q + 1) * 4, i * 128:(i + 1) * 128]
            if q % 2 == 0 or all_act:
                kb.op("act", lambda e, pt=pt, oap=oap: e.activation(out=oap, in_=pt[:], func=AF.Identity), reads=[pt], writes=[hTbuf])
            else:
                kb.op("dve", lambda e, pt=pt, oap=oap: e.tensor_copy(out=oap, in_=pt[:]), reads=[pt], writes=[hTbuf])

    def _hread(self, hTb, t0, n):
        return hTb[t0 // 128:(t0 + n + 127) // 128]

    def _wload(self, wbuf, src3):
        kc = src3.shape[1]
        h = kc // 2
        self.kb.dma("pool", wbuf[:, 0:h, 0:src3.shape[2]], src3[:, 0:h, :], writes=[wbuf])
        self.kb.dma("pool", wbuf[:, h:kc, 0:src3.shape[2]], src3[:, h:kc, :], writes=[wbuf])

    def _wload_hw(self, stg, dst_ap, dst_buf, src_ap, n_el_shape=None):
        kb = self.kb
        sb_ = stg.next()
        self._hwq = getattr(self, "_hwq", 0) + 1
        qs = getattr(self, "_hw_queues", ("sp",))
        q = qs[self._hwq % len(qs)]
        shp = src_ap.shape
        if len(shp) == 3:
            sv = sb_[:, 0:shp[1], 0:shp[2]]
        else:
            sv = sb_[:, 0:shp[1]]
        kb.dma(q, sv, src_ap, writes=[sb_])
        engs = getattr(self, "_cast_engs", ("pool", "act"))
        ce = engs[self._hwq % len(engs)]
        if ce == "act":
            kb.op("act", lambda e: e.activation(out=dst_ap, in_=sv, func=AF.Identity), reads=[sb_], writes=[dst_buf])
        else:
            kb.op(ce, lambda e: e.tensor_copy(out=dst_ap, in_=sv), reads=[sb_], writes=[dst_buf])

    def _wload2(self, stg, wbuf, src3):
        kc = src3.shape[1]
        h = kc // 2
        n = src3.shape[2]
        self._wload_hw(stg, wbuf[:, 0:h, 0:n], wbuf, src3[:, 0:h, :])
        self._wload_hw(stg, wbuf[:, h:kc, 0:n], wbuf, src3[:, h:kc, :])

    def _pipeline(self, tasks):
        tasks[0][0]()
        for i, (ld, cp) in enumerate(tasks):
            if i + 1 < len(tasks):
                tasks[i + 1][0]()
            cp()

    def _rope_proj(self, st, wA, wB, jj, hT, hTb, blk, cs, sn, pss, tmps, out_ap, out_buf):
        kb = self.kb
        t0, bs = blk
        pA, pB = pss.next(), pss.next()
        hr = self._hread(hTb, t0, bs)
        kb.mm_group(pA, pA[:, 0:bs], [wA[:, k, jj * 128:(jj + 1) * 128] for k in range(KC)],
                    [hT[:, k, t0:t0 + bs] for k in range(KC)], [wA] + hr)
        kb.mm_group(pB, pB[:, 0:bs], [wB[:, k, jj * 128:(jj + 1) * 128] for k in range(KC)],
                    [hT[:, k, t0:t0 + bs] for k in range(KC)], [wB] + hr)
        t1, t2 = tmps.next(), tmps.next()
        kb.op("dve", lambda e: e.tensor_tensor(out=t1[:, 0:bs], in0=pA[:, 0:bs], in1=cs[:, t0:t0 + bs], op=ALU.mult),
              reads=[pA, cs], writes=[t1])
        kb.op("dve", lambda e: e.tensor_tensor(out=t2[:, 0:bs], in0=pB[:, 0:bs], in1=sn[:, t0:t0 + bs], op=ALU.mult),
              reads=[pB, sn], writes=[t2])
        kb.op("pool", lambda e: e.tensor_tensor(out=out_ap, in0=t1[:, 0:bs], in1=t2[:, 0:bs], op=ALU.add),
              reads=[t1, t2], writes=[out_buf])

    def ph_A(self, l):
        cfg, kb = self.cfg, self.kb
        T = cfg.T
        w_in = self.din("w_in", (self.wdep, D, IN_W))
        w_qkp = self.din("w_qkp", (self.wdep, D, 2048))
        ropec = self.din("rope_cos", (128, T))
        ropes = self.din("rope_sin", (128, T))
        mk = self.dout if self.xchg_ext else self.dcc
        KTx = [mk("KT_x%d" % h, (128, T), BF16) for h in range(8)]
        Vx = [mk("V_x%d" % h, (T, 128), BF16) for h in range(8)]
        Hx = mk("halo_x", (8 * 128, 32), F32).rearrange("(h p) t -> h p t", p=128)
        wl = w_in[self.li(l)].rearrange("(kc p) n -> p kc n", p=128)
        wpl = w_qkp[self.li(l)].rearrange("(kc p) n -> p kc n", p=128)
        with ExitStack() as st:
            if self.fused:
                self._hT_stack = ExitStack()
                hT, hTb = self._make_hT(self._hT_stack, l, 0, 1, "hT_A")
                self._hT_keep = (l, hT, hTb)
            else:
                hT, hTb = self._make_hT(st, l, 0, 1, "hT_A")
            cs = self._sb(st, "cs", (128, T), F32)
            sn = self._sb(st, "sn", (128, T), F32)
            kb.dma("sp", cs[:], ropec, writes=[cs])
            kb.dma("sp", sn[:], ropes, writes=[sn])
            wg = Rot([self._sb(st, "wg%d" % i, (128, KC, 512), BF16) for i in range(4)])
            stg = Rot([self._sb(st, "stg%d" % i, (128, 8, 512), F32) for i in range(2)])
            pss = Rot([self._ps(st, "ps%d" % i) for i in range(8)])
            tmps = Rot([self._sb(st, "rt%d" % i, (128, 512), F32, dma=False) for i in range(4)])
            kt = Rot([self._sb(st, "kt%d" % i, (128, 512), BF16) for i in range(3)])
            vt = Rot([self._sb(st, "vt%d" % i, (128, 512), BF16) for i in range(3)])
            hl = Rot([self._sb(st, "hl%d" % i, (128, 32), F32) for i in range(2)])
            ranges = [0, cfg.LAT - 8, cfg.LAT, T - 8]
            tasks = []
            for g in range(2):
                hold = {}

                def ldK(g=g, hold=hold):
                    hold["A"], hold["B"] = wg.next(), wg.next()
                    self._wload2(stg, hold["A"], wl[:, :, K_OFF + g * 512:K_OFF + (g + 1) * 512])
                    self._wload2(stg, hold["B"], wpl[:, :, 1024 + g * 512:1024 + (g + 1) * 512])

                def cpK(g=g, hold=hold):
                    wA, wB = hold["A"], hold["B"]
                    for jj in range(4):
                        h = g * 4 + jj
                        for blk in cfg.blocks:
                            t0, bs = blk
                            o = kt.next()
                            self._rope_proj(st, wA, wB, jj, hT, hTb, blk, cs, sn, pss, tmps, o[:, 0:bs], o)
                            kb.dma("sp", KTx[h][:, t0:t0 + bs], o[:, 0:bs], reads=[o])
                tasks.append((ldK, cpK))
            for g in range(2):
                hold = {}

                def ldV(g=g, hold=hold):
                    hold["v"] = wg.next()
                    self._wload2(stg, hold["v"], wl[:, :, V_OFF + g * 512:V_OFF + (g + 1) * 512])

                def cpV(g=g, hold=hold):
                    wv = hold["v"]
                    for i in range(cfg.NT):
                        ps = pss.next()
                        kb.mm_group(ps, ps[:], [hT[:, k, i * 128:(i + 1) * 128] for k in range(KC)],
                                    [wv[:, k, :] for k in range(KC)], [wv, hTb[i]])
                        o = vt.next()
                        kb.op("act", lambda e, o=o, ps=ps: e.activation(out=o[:], in_=ps[:], func=AF.Identity), reads=[ps], writes=[o])
                        for hh in range(4):
                            kb.dma("sp", Vx[g * 4 + hh][i * 128:(i + 1) * 128, :], o[:, hh * 128:(hh + 1) * 128], reads=[o])
                tasks.append((ldV, cpV))
            for g in range(2):
                hold = {}

                def ldC(g=g, hold=hold):
                    hold["c"] = wg.next()
                    self._wload2(stg, hold["c"], wl[:, :, C_OFF + g * 512:C_OFF + (g + 1) * 512])

                def cpC(g=g, hold=hold):
                    wc = hold["c"]
                    for jj in range(4):
                        ps = pss.next()
                        for r, t0 in enumerate(ranges):
                            kb.mm_group(ps, ps[:, r * 8:(r + 1) * 8], [wc[:, k, jj * 128:(jj + 1) * 128] for k in range(KC)],
                                        [hT[:, k, t0:t0 + 8] for k in range(KC)], [wc] + self._hread(hTb, t0, 8))
                        o = hl.next()
                        kb.op("dve", lambda e, o=o, ps=ps: e.tensor_copy(out=o[:], in_=ps[:, 0:32]), reads=[ps], writes=[o])
                        kb.dma("sp", Hx[g * 4 + jj, :, :], o[:], reads=[o])
                tasks.append((ldC, cpC))
            self._pipeline(tasks)
            kb.run_stage("A")

    def ph_B(self, l):
        cfg = self.cfg
        last = (l == cfg.depth - 1)
        self.blocksB = cfg.lat_blocks if last else cfg.blocks
        self.tilesB = cfg.NTL if last else cfg.NT
        with ExitStack() as st:
            if self.fused and getattr(self, "_hT_keep", None) is not None and self._hT_keep[0] == l:
                _, hT, hTb = self._hT_keep
                self._hT_keep = None
                st.enter_context(self._hT_stack)
            else:
                hT, hTb = self._make_hT(st, l, 0, 1, "hT_B")
            self._attn(st, l, hT, hTb, last)
            self._gmlp(st, l, hT, hTb)
            self._poolbr(st, l, hT, hTb)
            self._merge(st, l, hT, hTb)
        self._outproj(l, last)
        self._ffn_up(l)
        self._ffn_down(l, last)

    def _yT(self):
        return self.dscr("yT", (3, 8, 128, self.cfg.T), BF16)

    def _gathered(self):
        cfg = self.cfg
        mk = self.din if self.xchg_ext else self.dcc
        return ([mk("KT_all%d" % h, (2 * 128, cfg.T), BF16).rearrange("(r p) t -> r p t", r=2) for h in range(8)],
                [mk("V_all%d" % h, (2 * cfg.T, 128), BF16).rearrange("(r t) d -> r t d", r=2) for h in range(8)],
                mk("halo_all", (2 * 8 * 128, 32), F32).rearrange("(r h p) t -> r h p t", r=2, p=128))

    def _attn(self, st0, l, hT, hTb, last):
        cfg, kb = self.cfg, self.kb
        T, NT = cfg.T, cfg.NT
        lam_init = 0.8 - 0.6 * math.exp(-0.3 * l)
        w_in = self.din("w_in", (self.wdep, D, IN_W))
        w_qkp = self.din("w_qkp", (self.wdep, D, 2048))
        ropec = self.din("rope_cos", (128, T))
        ropes = self.din("rope_sin", (128, T))
        lamqk = self.din("lam_qk", (self.wdep, 256))
        sublng = self.din("subln_g", (self.wdep, 128, 1))
        KTa, Va, _ = self._gathered()
        yT = self._yT()
        wl = w_in[self.li(l)].rearrange("(kc p) n -> p kc n", p=128)
        wpl = w_qkp[self.li(l)].rearrange("(kc p) n -> p kc n", p=128)
        with ExitStack() as st:
            cs = self._sb(st, "cs", (128, T), F32)
            sn = self._sb(st, "sn", (128, T), F32)
            kb.dma("sp", cs[:], ropec, writes=[cs])
            kb.dma("sp", sn[:], ropes, writes=[sn])
            lq = self._sb(st, "lq", (128, 4, 64), F32)
            kb.dma("sp", lq[:], lamqk[self.li(l), :].partition_broadcast(128).rearrange("p (a b) -> p a b", a=4), writes=[lq])
            lp = self._sb(st, "lp", (128, 2, 64), F32, dma=False)
            ls = self._sb(st, "ls", (128, 2), F32, dma=False)
            nlam = self._sb(st, "nlam", (128, 1), F32, dma=False)
            kb.op("dve", lambda e: e.tensor_tensor(out=lp[:, 0, :], in0=lq[:, 0, :], in1=lq[:, 1, :], op=ALU.mult), reads=[lq], writes=[lp])
            kb.op("dve", lambda e: e.tensor_tensor(out=lp[:, 1, :], in0=lq[:, 2, :], in1=lq[:, 3, :], op=ALU.mult), reads=[lq, lp], writes=[lp])
            kb.op("dve", lambda e: e.tensor_reduce(out=ls[:], in_=lp[:], axis=mybir.AxisListType.X, op=ALU.add), reads=[lp], writes=[ls])
            kb.op("act", lambda e: e.activation(out=ls[:], in_=ls[:], func=AF.Exp), reads=[ls], writes=[ls])
            kb.op("dve", lambda e: e.tensor_tensor(out=nlam[:], in0=ls[:, 1:2], in1=ls[:, 0:1], op=ALU.subtract), reads=[ls], writes=[nlam])
            kb.op("dve", lambda e: e.tensor_scalar(out=nlam[:], in0=nlam[:], scalar1=-lam_init, scalar2=None, op0=ALU.add),
                  reads=[nlam], writes=[nlam])
            gsub = self._sb(st, "gsub", (128, 1), F32)
            kb.dma("sp", gsub[:], sublng[self.li(l)], writes=[gsub])
            kb.op("dve", lambda e: e.tensor_scalar(out=gsub[:], in0=gsub[:], scalar1=(1.0 - lam_init), scalar2=None, op0=ALU.mult),
                  reads=[gsub], writes=[gsub])
            onesb = self._sb(st, "onesb", (128, 128), BF16, dma=False)
            onesf = self._sb(st, "onesf", (128, 128), F32, dma=False)
            kb.op("pool", lambda e: e.memset(onesb[:], 1.0), writes=[onesb])
            kb.op("pool", lambda e: e.memset(onesf[:], 1.0), writes=[onesf])

            wg = Rot([self._sb(st, "wg%d" % i, (128, KC, 512), BF16) for i in range(2)])
            qT = Rot([self._sb(st, "qT%d" % i, (128, T), BF16, dma=False) for i in range(2)])
            ktb = Rot([self._sb(st, "ktb%d" % i, (128, 2, T), BF16) for i in range(2)])
            vb = Rot([self._sb(st, "vb%d" % i, (128, 2 * NT, 128), BF16) for i in range(2)])
            pst2 = Rot([self._ps(st, "pst2_%d" % i, (128, 1024)) for i in range(2)])
            pacc = [self._ps(st, "pacc%d" % i) for i in range(4)]
            pst = pst2
            E2 = Rot([self._sb(st, "E%d" % i, (128, 1024), BF16, dma=False) for i in range(3)])
            tmps = Rot([self._sb(st, "rt%d" % i, (128, 512), F32, dma=False) for i in range(4)])
            ep = [self._sb(st, "ep%d" % i, (128, 512), F32, dma=False) for i in range(4)]
            yab = Rot([self._sb(st, "yab%d" % i, (128, 512), BF16) for i in range(2)])
            wA = wB = None
            pend = [None]
            for h in range(8):
                g, jj = h // 4, h % 4
                if jj == 0:
                    wA, wB = wg.next(), wg.next()
                    self._wload(wA, wl[:, :, g * 512:(g + 1) * 512])
                    self._wload(wB, wpl[:, :, g * 512:(g + 1) * 512])
                q = qT.next()
                for blk in self.blocksB:
                    t0, bs = blk
                    self._rope_proj(st, wA, wB, jj, hT, hTb, blk, cs, sn, pst, tmps, q[:, t0:t0 + bs], q)
                kt_, v_ = ktb.next(), vb.next()
                for r in range(2):
                    kb.dma("sp", kt_[:, r, :], KTa[h][r, :, :], writes=[kt_])
                    kb.dma("sp", v_[:, r * NT:(r + 1) * NT, :],
                           Va[h][r, :, :].rearrange("(i p) d -> p i d", p=128), writes=[v_])
                for blk in self.blocksB:
                    t0, bs = blk
                    is_ctx = t0 >= cfg.LAT
                    if is_ctx:
                        chunks = [(0, cfg.NTL), (1, cfg.NTL)]
                    else:
                        chunks = [(r, i) for r in range(2) for i in range(NT)]
                    nck = len(chunks)

                    def issue_S(ci, kt_=kt_, q=q, t0=t0, bs=bs, chunks=chunks):
                        r, i = chunks[ci]
                        sp2 = pst2.next()
                        for m in range(2):
                            kb.op("pe", lambda e, sp2=sp2, m=m, r=r, i=i:
                                  e.matmul(sp2[:, m * 512:m * 512 + bs], lhsT=kt_[m * 64:(m + 1) * 64, r, i * 128:(i + 1) * 128],
                                           rhs=q[m * 64:(m + 1) * 64, t0:t0 + bs], start=True, stop=True),
                                  reads=[kt_, q], writes=[sp2], inc=(m == 1))
                        return sp2

                    sp_next = issue_S(0)
                    for ci, (r, i) in enumerate(chunks):
                        kidx = r * NT + i
                        sp2 = sp_next
                        E = E2.next()
                        kb.op("act", lambda e, E=E, sp2=sp2, bs=bs: e.activation(
                            out=E[:].rearrange("p (m n) -> p m n", m=2)[:, :, 0:bs],
                            in_=sp2[:].rearrange("p (m n) -> p m n", m=2)[:, :, 0:bs], func=AF.Exp, scale=A_SCALE),
                            reads=[sp2], writes=[E])
                        if ci + 1 < nck:
                            sp_next = issue_S(ci + 1)
                        if ci == 2 and pend[0] is not None:
                            pend[0]()
                            pend[0] = None
                        for m in range(2):
                            kb.op("pe", lambda e, m=m, v_=v_, kidx=kidx, E=E, bs=bs, ci=ci, nck=nck:
                                  e.matmul(pacc[m][:, 0:bs], lhsT=v_[:, kidx, :], rhs=E[:, m * 512:m * 512 + bs], start=(ci == 0), stop=(ci == nck - 1)),
                                  reads=[v_, E], writes=[pacc[m]], inc=False)
                            kb.op("pe", lambda e, m=m, E=E, bs=bs, ci=ci, nck=nck:
                                  e.matmul(pacc[2 + m][:, 0:bs], lhsT=onesb[:], rhs=E[:, m * 512:m * 512 + bs], start=(ci == 0), stop=(ci == nck - 1)),
                                  reads=[onesb, E], writes=[pacc[2 + m]], inc=(m == 1))
                    if pend[0] is not None:
                        pend[0]()
                        pend[0] = None
                    r0, r1, a0, a1 = ep
                    kb.op("dve", lambda e, bs=bs: e.reciprocal(out=r0[:, 0:bs], in_=pacc[2][:, 0:bs]), reads=[pacc[2]], writes=[r0])
                    kb.op("dve", lambda e, bs=bs: e.reciprocal(out=r1[:, 0:bs], in_=pacc[3][:, 0:bs]), reads=[pacc[3]], writes=[r1])
                    kb.op("dve", lambda e, bs=bs: e.tensor_tensor(out=a0[:, 0:bs], in0=pacc[0][:, 0:bs], in1=r0[:, 0:bs], op=ALU.mult),
                          reads=[pacc[0], r0], writes=[a0])
                    kb.op("dve", lambda e, bs=bs: e.tensor_tensor(out=a1[:, 0:bs], in0=pacc[1][:, 0:bs], in1=r1[:, 0:bs], op=ALU.mult),
                          reads=[pacc[1], r1], writes=[a1])
                    kb.op("dve", lambda e, bs=bs: e.scalar_tensor_tensor(out=a0[:, 0:bs], in0=a1[:, 0:bs], scalar=nlam[:, 0:1], in1=a0[:, 0:bs],
                                                                        op0=ALU.mult, op1=ALU.add),
                          reads=[a1, nlam, a0], writes=[a0])
                    kb.op("pool", lambda e, bs=bs: e.tensor_tensor(out=a1[:, 0:bs], in0=a0[:, 0:bs], in1=a0[:, 0:bs], op=ALU.mult),
                          reads=[a0], writes=[a1])

                    def part2(bs=bs, t0=t0, h=h):
                        sps = pst.next()
                        kb.op("pe", lambda e, sps=sps, bs=bs: e.matmul(sps[:, 0:bs], lhsT=onesf[:], rhs=a1[:, 0:bs], start=True, stop=True),
                              reads=[onesf, a1], writes=[sps])
                        kb.op("dve", lambda e, sps=sps, bs=bs: e.tensor_scalar(out=r0[:, 0:bs], in0=sps[:, 0:bs], scalar1=1.0 / 128, scalar2=EPS,
                                                                               op0=ALU.mult, op1=ALU.add), reads=[sps], writes=[r0])
                        kb.op("act", lambda e, bs=bs: e.activation(out=r0[:, 0:bs], in_=r0[:, 0:bs], func=AF.Sqrt), reads=[r0], writes=[r0])
                        kb.op("dve", lambda e, bs=bs: e.reciprocal(out=r1[:, 0:bs], in_=r0[:, 0:bs]), reads=[r0], writes=[r1])
                        kb.op("dve", lambda e, bs=bs: e.tensor_tensor(out=a0[:, 0:bs], in0=a0[:, 0:bs], in1=r1[:, 0:bs], op=ALU.mult),
                              reads=[a0, r1], writes=[a0])
                        yo = yab.next()
                        kb.op("dve", lambda e, yo=yo, bs=bs: e.tensor_scalar(out=yo[:, 0:bs], in0=a0[:, 0:bs], scalar1=gsub[:, 0:1], scalar2=None, op0=ALU.mult),
                              reads=[a0, gsub], writes=[yo])
                        kb.dma("sp", yT[0, h, :, t0:t0 + bs], yo[:, 0:bs], reads=[yo])
                    pend[0] = part2
            if pend[0] is not None:
                pend[0]()
                pend[0] = None
            kb.run_stage("attn")

    def _gmlp(self, st0, l, hT, hTb):
        cfg, kb = self.cfg, self.kb
        T = cfg.T
        w_in = self.din("w_in", (self.wdep, D, IN_W))
        wsT = self.din("wsT", (self.wdep, 128, 8, 128))
        bsp = self.din("b_spatial", (self.wdep, 1024))
        lng = self.din("gmlp_ln_g", (self.wdep, 1024))
        lnb = self.din("gmlp_ln_b", (self.wdep, 1024))
        yT = self._yT()
        wl = w_in[self.li(l)].rearrange("(kc p) n -> p kc n", p=128)
        with ExitStack() as st:
            wu = self._sb(st, "wu", (128, KC, 1024), BF16)
            wv = self._sb(st, "wv", (128, KC, 1024), BF16)
            wug = [wu, kb.buf(wu.t, dma=True, name="wu1")]
            wvg = [wv, kb.buf(wv.t, dma=True, name="wv1")]
            for g in range(2):
                kb.dma("pool", wu[:, :, g * 512:(g + 1) * 512], wl[:, :, BU_OFF + g * 512:BU_OFF + (g + 1) * 512], writes=[wug[g]])
            for g in range(2):
                kb.dma("pool", wv[:, :, g * 512:(g + 1) * 512], wl[:, :, BV_OFF + g * 512:BV_OFF + (g + 1) * 512], writes=[wvg[g]])
            ws = self._sb(st, "ws", (128, 8, 128), BF16)
            kb.dma("pool", ws[:], wsT[self.li(l)], writes=[ws])
            bsb = self._sb(st, "bsb", (128, 8, 128), F32)
            kb.dma("sp", bsb[:], bsp[self.li(l), :].partition_broadcast(128).rearrange("p (g i) -> p g i", g=8), writes=[bsb])
            gbc = self._sb(st, "gbc", (128, 1024), F32)
            bbc = self._sb(st, "bbc", (128, 1024), F32)
            self._bcast_load(gbc, lng[self.li(l), :])
            self._bcast_load(bbc, lnb[self.li(l), :])
            ub = Rot([self._sb(st, "ub%d" % i, (128, 8, 512), F32, dma=False) for i in range(1)])
            pss = Rot([self._ps(st, "ps%d" % i) for i in range(8)])
            vg = Rot([self._sb(st, "vg%d" % i, (128, 1024), F32, dma=False) for i in range(2)])
            vln = Rot([self._sb(st, "vln%d" % i, (128, 1024), BF16, dma=False) for i in range(3)])
            stt = Rot([self._sb(st, "stt%d" % i, (128, 2, 6), F32, dma=False) for i in range(2)])
            mv = Rot([self._sb(st, "mv%d" % i, (128, 2), F32, dma=False) for i in range(2)])
            rs = Rot([self._sb(st, "rs%d" % i, (128, 1), F32, dma=False) for i in range(2)])
            tm = Rot([self._sb(st, "tm%d" % i, (128, 1), F32, dma=False) for i in range(2)])
            stmp = Rot([self._sb(st, "stmp%d" % i, (128, 4, 128), F32, dma=False) for i in range(2)])
            ybt = Rot([self._sb(st, "ybt%d" % i, (128, 8, 128), BF16) for i in range(2)])
            for blk in self.blocksB:
                t0, bs = blk
                u = ub.next()
                hr = self._hread(hTb, t0, bs)
                for c8 in range(8):
                    ps = pss.next()
                    kb.mm_group(ps, ps[:, 0:bs], [wu[:, k, c8 * 128:(c8 + 1) * 128] for k in range(KC)],
                                [hT[:, k, t0:t0 + bs] for k in range(KC)], [wug[c8 // 4]] + hr)
                    kb.op("act", lambda e, u=u, c8=c8, ps=ps, bs=bs: e.activation(out=u[:, c8, 0:bs], in_=ps[:, 0:bs], func=AF.Gelu),
                          reads=[ps], writes=[u])
                def stA(ti, t0=t0):
                    i = t0 // 128 + ti
                    v_ = vg.next()
                    for n in range(2):
                        ps = pss.next()
                        kb.mm_group(ps, ps[:], [hT[:, k, i * 128:(i + 1) * 128] for k in range(KC)],
                                    [wv[:, k, n * 512:(n + 1) * 512] for k in range(KC)], [wvg[n], hTb[i]])
                        kb.op("act", lambda e, v_=v_, n=n, ps=ps: e.activation(out=v_[:, n * 512:(n + 1) * 512], in_=ps[:], func=AF.Gelu),
                              reads=[ps], writes=[v_])
                    s_, m_, r_, t_ = stt.next(), mv.next(), rs.next(), tm.next()
                    for n in range(2):
                        kb.op("dve", lambda e, s_=s_, v_=v_, n=n: e.bn_stats(out=s_[:, n, :], in_=v_[:, n * 512:(n + 1) * 512]),
                              reads=[v_], writes=[s_])
                    kb.op("dve", lambda e, s_=s_, m_=m_: e.bn_aggr(out=m_[:], in_=s_[:].rearrange("p a b -> p (a b)")), reads=[s_], writes=[m_])
                    self._rstd(m_[:, 1:2], r_, t_, [m_])
                    kb.op("dve", lambda e, v_=v_, m_=m_, r_=r_: e.tensor_scalar(out=v_[:], in0=v_[:], scalar1=m_[:, 0:1], scalar2=r_[:, 0:1],
                                                                              op0=ALU.subtract, op1=ALU.mult), reads=[v_, m_, r_], writes=[v_])
                    kb.op("dve", lambda e, v_=v_: e.tensor_tensor(out=v_[:], in0=v_[:], in1=gbc[:], op=ALU.mult), reads=[v_, gbc], writes=[v_])
                    vl = vln.next()
                    kb.op("pool", lambda e, v_=v_, vl=vl: e.tensor_tensor(out=vl[:], in0=v_[:], in1=bbc[:], op=ALU.add), reads=[v_, bbc], writes=[vl])
                    return vl

                def stB(ti, vl, t0=t0, u=u):
                    i = t0 // 128 + ti
                    yb = ybt.next()
                    for half in range(2):
                        ps = pss.next()
                        for gg in range(4):
                            g8 = half * 4 + gg
                            kb.op("pe", lambda e, ps=ps, gg=gg, g8=g8, vl=vl: e.matmul(ps[:, gg * 128:(gg + 1) * 128], lhsT=vl[:, g8 * 128:(g8 + 1) * 128],
                                                                                     rhs=ws[:, g8, :], start=True, stop=True),
                                  reads=[vl, ws], writes=[ps], inc=(gg == 3))
                        sm = stmp.next()
                        kb.op("dve", lambda e, sm=sm, ps=ps, half=half: e.tensor_tensor(out=sm[:], in0=ps[:].rearrange("p (g i) -> p g i", g=4),
                                                                                        in1=bsb[:, half * 4:(half + 1) * 4, :], op=ALU.add),
                              reads=[ps, bsb], writes=[sm])
                        kb.op("pool", lambda e, sm=sm, yb=yb, u=u, half=half, ti=ti: e.tensor_tensor(
                            out=yb[:, half * 4:(half + 1) * 4, :], in0=sm[:], in1=u[:, half * 4:(half + 1) * 4, ti * 128:(ti + 1) * 128], op=ALU.mult),
                            reads=[sm, u], writes=[yb])
                    kb.dma("sp", yT[1, :, :, i * 128:(i + 1) * 128].rearrange("c p t -> p c t"), yb[:], reads=[yb])

                ntl = bs // 128
                vl_next = stA(0)
                for ti in range(ntl):
                    vl_cur = vl_next
                    if ti + 1 < ntl:
                        vl_next = stA(ti + 1)
                    stB(ti, vl_cur)
            kb.run_stage("gmlp")

    def _poolbr(self, st0, l, hT, hTb):
        cfg, kb = self.cfg, self.kb
        T, LAT = cfg.T, cfg.LAT
        w_in = self.din("w_in", (self.wdep, D, IN_W))
        w_pool = self.din("w_pool", (self.wdep, 4, 256, 256))
        pscale = self.din("pool_scaleT", (self.wdep, 128, 8))
        hmask = self.din("hmask", (128, 2))
        corr = self.din("pcorr", (128, 2, 4, 16))
        _, _, Ha = self._gathered()
        yT = self._yT()
        wl = w_in[self.li(l)].rearrange("(kc p) n -> p kc n", p=128)
        has_ctx = len(self.blocksB) > len(cfg.lat_blocks)
        with ExitStack() as st:
            wg = Rot([self._sb(st, "wg%d" % i, (128, KC, 512), BF16) for i in range(2)])
            wp = self._sb(st, "wp", (128, 4, 2, 256), BF16)
            kb.dma("pool", wp[:], w_pool[self.li(l)].rearrange("g (cc p) d -> p g cc d", p=128), writes=[wp])
            psc = self._sb(st, "psc", (128, 8), F32)
            kb.dma("sp", psc[:], pscale[self.li(l)], writes=[psc])
            hm = self._sb(st, "hm", (128, 2), F32)
            kb.dma("sp", hm[:], hmask, writes=[hm])
            cr = self._sb(st, "cr", (128, 2, 4, 16), F32)
            kb.dma("sp", cr[:], corr, writes=[cr])
            zc = [self._sb(st, "zc%d" % i, (128, LAT + 16), F32, dma=False) for i in range(2)]
            zx = [self._sb(st, "zx%d" % i, (128, 128 + 16), F32, dma=False) for i in range(2)]
            dT = [self._sb(st, "dT%d" % i, (128, T), BF16, dma=False) for i in range(2)]
            Aa = self._sb(st, "Aa", (128, LAT + 16), F32, dma=False)
            Ab = self._sb(st, "Ab", (128, LAT + 16), F32, dma=False)
            hl = Rot([self._sb(st, "hl%d" % i, (128, 2, 32), F32) for i in range(2)])
            pss = Rot([self._ps(st, "ps%d" % i) for i in range(8)])
            yo = Rot([self._sb(st, "yo%d" % i, (128, 512), BF16) for i in range(3)])
            wc = None
            for g4 in range(4):
                w = WINS[g4]
                if g4 % 2 == 0:
                    wc = wg.next()
                    self._wload(wc, wl[:, :, C_OFF + (g4 // 2) * 512:C_OFF + (g4 // 2 + 1) * 512])
                for cc in range(2):
                    c8 = 2 * g4 + cc
                    jj = c8 % 4
                    z, zz, d_ = zc[cc], zx[cc], dT[cc]
                    for blk in self.blocksB:
                        t0, bs = blk
                        ps = pss.next()
                        kb.mm_group(ps, ps[:, 0:bs], [wc[:, k, jj * 128:(jj + 1) * 128] for k in range(KC)],
                                    [hT[:, k, t0:t0 + bs] for k in range(KC)], [wc] + self._hread(hTb, t0, bs))
                        if t0 < LAT:
                            kb.op("act", lambda e, z=z, ps=ps, t0=t0, bs=bs: e.activation(out=z[:, 8 + t0:8 + t0 + bs], in_=ps[:, 0:bs], func=AF.Identity),
                                  reads=[ps], writes=[z])
                        else:
                            kb.op("act", lambda e, zz=zz, ps=ps: e.activation(out=zz[:, 8:136], in_=ps[:, 0:128], func=AF.Identity),
                                  reads=[ps], writes=[zz])
                    h_ = hl.next()
                    kb.dma("sp", h_[:], Ha[:, c8, :, :].rearrange("r p t -> p r t"), writes=[h_])
                    kb.op("dve", lambda e, z=z, h_=h_: e.tensor_scalar(out=z[:, 0:8], in0=h_[:, 0, 8:16], scalar1=hm[:, 0:1], scalar2=None, op0=ALU.mult),
                          reads=[h_, hm], writes=[z])
                    kb.op("dve", lambda e, z=z, h_=h_: e.tensor_scalar(out=z[:, 8 + LAT:16 + LAT], in0=h_[:, 1, 0:8], scalar1=hm[:, 1:2], scalar2=None, op0=ALU.mult),
                          reads=[h_, hm], writes=[z])
                    self._pool1d(z, LAT, w, g4, 0, Aa, Ab, cr, d_, 0)
                    if has_ctx:
                        kb.op("dve", lambda e, zz=zz, h_=h_: e.tensor_scalar(out=zz[:, 0:8], in0=h_[:, 0, 24:32], scalar1=hm[:, 0:1], scalar2=None, op0=ALU.mult),
                              reads=[h_, hm], writes=[zz])
                        kb.op("dve", lambda e, zz=zz, h_=h_: e.tensor_scalar(out=zz[:, 136:144], in0=h_[:, 1, 16:24], scalar1=hm[:, 1:2], scalar2=None, op0=ALU.mult),
                              reads=[h_, hm], writes=[zz])
                        self._pool1d(zz, 128, w, g4, 1, Aa, Ab, cr, d_, LAT)
                for dd in range(2):
                    co = 2 * g4 + dd
                    for blk in self.blocksB:
                        t0, bs = blk
                        ps = pss.next()
                        kb.mm_group(ps, ps[:, 0:bs], [wp[:, g4, cc, dd * 128:(dd + 1) * 128] for cc in range(2)],
                                    [dT[cc][:, t0:t0 + bs] for cc in range(2)], [wp, dT[0], dT[1]])
                        o = yo.next()
                        kb.op("act", lambda e, o=o, ps=ps, bs=bs, co=co: e.activation(out=o[:, 0:bs], in_=ps[:, 0:bs], func=AF.Identity, scale=psc[:, co:co + 1]),
                              reads=[ps, psc], writes=[o])
                        kb.dma("sp", yT[2, co, :, t0:t0 + bs], o[:, 0:bs], reads=[o])
            kb.run_stage("pool")

    def _pool1d(self, z, n, w, g4, seq, Aa, Ab, cr, d_, dcol):
        kb = self.kb
        src, length, step = z, n + 16, 1
        bufs = [Aa, Ab]
        bi = 0
        while step < w:
            dst = bufs[bi % 2]
            bi += 1
            nl = length - step
            kb.op("pool", lambda e, dst=dst, src=src, nl=nl, step=step: e.tensor_tensor(out=dst[:, 0:nl], in0=src[:, 0:nl], in1=src[:, step:step + nl], op=ALU.add),
                  reads=[src], writes=[dst])
            src, length, step = dst, nl, step * 2
        off = 8 - w // 2
        other = bufs[bi % 2]
        kb.op("dve", lambda e: e.tensor_scalar(out=other[:, 0:n], in0=src[:, off:off + n], scalar1=1.0 / w, scalar2=None, op0=ALU.mult),
              reads=[src], writes=[other])
        kb.op("dve", lambda e: e.tensor_tensor(out=other[:, 0:8], in0=other[:, 0:8], in1=cr[:, seq, g4, 0:8], op=ALU.mult),
              reads=[other, cr], writes=[other])
        kb.op("dve", lambda e: e.tensor_tensor(out=other[:, n - 8:n], in0=other[:, n - 8:n], in1=cr[:, seq, g4, 8:16], op=ALU.mult),
              reads=[other, cr], writes=[other])
        kb.op("dve", lambda e: e.tensor_tensor(out=d_[:, dcol:dcol + n], in0=other[:, 0:n], in1=z[:, 8:8 + n], op=ALU.subtract),
              reads=[other, z], writes=[d_])

    def _merge(self, st0, l, hT, hTb):
        cfg, kb = self.cfg, self.kb
        T = cfg.T
        w_in = self.din("w_in", (self.wdep, D, IN_W))
        w_br = self.din("w_branch", (self.wdep, 3, 1024, D))
        yT = self._yT()
        mT = self.dscr("mT", (KC, 128, T), BF16)
        wl = w_in[self.li(l)].rearrange("(kc p) n -> p kc n", p=128)
        wbl = w_br[self.li(l)].rearrange("n (kc p) d -> p n kc d", p=128)
        nb = len(self.blocksB)
        sbs = [self.blocksB[:max(1, nb // 2)], self.blocksB[max(1, nb // 2):]]
        sbs = [s for s in sbs if s]
        maxw = max(sum(b[1] for b in s) for s in sbs)
        with ExitStack() as st:
            ysb = self._sb(st, "ysb", (128, 3, 8, maxw), BF16)
            wb = Rot([self._sb(st, "wb%d" % i, (128, 3, 8, 128), BF16) for i in range(2)])
            wgt = Rot([self._sb(st, "wgt%d" % i, (128, 3, KC, 128), BF16) for i in range(2)])
            pss = Rot([self._ps(st, "ps%d" % i) for i in range(8)])
            sg = Rot([self._sb(st, "sg%d" % i, (128, 512), F32, dma=False) for i in range(3)])
            pr = Rot([self._sb(st, "pr%d" % i, (128, 512), F32, dma=False) for i in range(3)])
            mo = Rot([self._sb(st, "mo%d" % i, (128, 512), BF16) for i in range(3)])
            stgg = Rot([self._sb(st, "stgg%d" % i, (128, KC, 128), F32) for i in range(2)])
            stgb = Rot([self._sb(st, "stgb%d" % i, (128, 8, 128), F32) for i in range(2)])
            self._cast_engs = ("act", "dve")
            self._hw_queues = ("sp", "sp", "act")
            for sbk in sbs:
                s0 = sbk[0][0]
                sw = sum(b[1] for b in sbk)
                for n in range(3):
                    kb.dma("sp", ysb[:, n, :, 0:sw], yT[n, :, :, s0:s0 + sw].rearrange("c p t -> p c t"), writes=[ysb])
                tasks = []
                for dc in range(KC):
                    hold = {}

                    def ld(dc=dc, hold=hold):
                        hold["b"], hold["g"] = wb.next(), wgt.next()
                        for n in range(3):
                            self._wload_hw(stgb, hold["b"][:, n, :, :], hold["b"], wbl[:, n, :, dc * 128:(dc + 1) * 128])
                            c0 = G_OFF + n * D + dc * 128
                            self._wload_hw(stgg, hold["g"][:, n, :, :], hold["g"], wl[:, :, c0:c0 + 128])

                    def cp(dc=dc, hold=hold, sbk=sbk, s0=s0):
                        wb_, wg_ = hold["b"], hold["g"]
                        for blk in sbk:
                            t0, bs = blk
                            lo = t0 - s0
                            hr = self._hread(hTb, t0, bs)
                            prods = []
                            for n in range(3):
                                pg, pp = pss.next(), pss.next()
                                kb.mm_group(pg, pg[:, 0:bs], [wg_[:, n, k, :] for k in range(KC)], [hT[:, k, t0:t0 + bs] for k in range(KC)], [wg_] + hr)
                                kb.mm_group(pp, pp[:, 0:bs], [wb_[:, n, k, :] for k in range(8)], [ysb[:, n, k, lo:lo + bs] for k in range(8)], [wb_, ysb])
                                s_ = sg.next()
                                kb.op("act", lambda e, s_=s_, pg=pg, bs=bs: e.activation(out=s_[:, 0:bs], in_=pg[:, 0:bs], func=AF.Sigmoid), reads=[pg], writes=[s_])
                                p_ = pr.next()
                                kb.op("dve", lambda e, p_=p_, s_=s_, pp=pp, bs=bs: e.tensor_tensor(out=p_[:, 0:bs], in0=pp[:, 0:bs], in1=s_[:, 0:bs], op=ALU.mult),
                                      reads=[pp, s_], writes=[p_])
                                prods.append(p_)
                            p0, p1, p2 = prods
                            kb.op("pool", lambda e, p0=p0, p1=p1, bs=bs: e.tensor_tensor(out=p0[:, 0:bs], in0=p0[:, 0:bs], in1=p1[:, 0:bs], op=ALU.add),
                                  reads=[p0, p1], writes=[p0])
                            o = mo.next()
                            kb.op("pool", lambda e, o=o, p0=p0, p2=p2, bs=bs: e.tensor_tensor(out=o[:, 0:bs], in0=p0[:, 0:bs], in1=p2[:, 0:bs], op=ALU.add),
                                  reads=[p0, p2], writes=[o])
                            kb.dma("sp", mT[dc, :, t0:t0 + bs], o[:, 0:bs], reads=[o])
                    tasks.append((ld, cp))
                self._pipeline(tasks)
            self._cast_engs = ("pool", "act")
            self._hw_queues = ("sp",)
            kb.run_stage("merge")

    def _ln_ep1(self, pss4, tmp):
        kb = self.kb
        for n in range(4):
            ps = pss4[n]
            sl = slice(n * 512, (n + 1) * 512)
            kb.op("act", lambda e, ps=ps, sl=sl: e.activation(out=tmp[:, sl], in_=ps[:], func=AF.Identity), reads=[ps], writes=[tmp])

    def _ln_ep2(self, G, xt, lng_bc, lnb_bc, tmp, stt, mv, rs, tm, extra_add=None):
        kb = self.kb
        if extra_add is not None:
            kb.op("dve", lambda e: e.tensor_tensor(out=tmp[:], in0=tmp[:], in1=extra_add[:], op=ALU.add), reads=[tmp, extra_add], writes=[tmp])
        kb.op("dve", lambda e: e.tensor_tensor(out=tmp[:], in0=tmp[:], in1=G[:], op=ALU.mult), reads=[tmp, G], writes=[tmp])
        kb.op("dve", lambda e: e.scalar_tensor_tensor(out=xt[:], in0=xt[:], scalar=ALPHA, in1=tmp[:], op0=ALU.mult, op1=ALU.add),
              reads=[xt, tmp], writes=[xt])
        for q in range(4):
            kb.op("dve", lambda e, q=q: e.bn_stats(out=stt[:, q, :], in_=xt[:, q * 512:(q + 1) * 512]), reads=[xt], writes=[stt])
        kb.op("dve", lambda e: e.bn_aggr(out=mv[:], in_=stt[:].rearrange("p a b -> p (a b)")), reads=[stt], writes=[mv])
        self._rstd(mv[:, 1:2], rs, tm, [mv])
        kb.op("dve", lambda e: e.tensor_scalar(out=xt[:], in0=xt[:], scalar1=mv[:, 0:1], scalar2=rs[:, 0:1], op0=ALU.subtract, op1=ALU.mult),
              reads=[xt, mv, rs], writes=[xt])
        kb.op("dve", lambda e: e.tensor_tensor(out=xt[:], in0=xt[:], in1=lng_bc[:], op=ALU.mult), reads=[xt, lng_bc], writes=[xt])
        kb.op("pool", lambda e: e.tensor_tensor(out=xt[:], in0=xt[:], in1=lnb_bc[:], op=ALU.add), reads=[xt, lnb_bc], writes=[xt])

    def _outproj(self, l, last):
        cfg, kb = self.cfg, self.kb
        T = cfg.T
        w_out = self.din("w_out", (self.wdep, D, D))
        ln1g = self.din("ln1_g", (self.wdep, D))
        ln1b = self.din("ln1_b", (self.wdep, D))
        mods = self._mods()
        xres = self._xres()
        mT = self.dscr("mT", (KC, 128, T), BF16)
        hfT = self.dscr("hfT", (KC, 128, T), BF16)
        with ExitStack() as st:
            idf, idb = self._ident(st)
            wo = self._sb(st, "wo", (128, KC, D), BF16)
            wol = w_out[self.li(l)].rearrange("(kc p) n -> p kc n", p=128)
            won = [wo] + [kb.buf(wo.t, dma=True, name="won%d" % g) for g in range(1, 4)]
            stg = Rot([self._sb(st, "stg%d" % i, (128, 8, 512), F32) for i in range(2)])
            self._cast_engs = ("act", "dve")
            for g in range(4):
                self._wload_hw(stg, wo[:, 0:8, g * 512:(g + 1) * 512], won[g], wol[:, 0:8, g * 512:(g + 1) * 512])
                self._wload_hw(stg, wo[:, 8:16, g * 512:(g + 1) * 512], won[g], wol[:, 8:16, g * 512:(g + 1) * 512])
            self._cast_engs = ("pool", "act")
            gm = self._sb(st, "gm", (128, D), F32)
            A2 = self._sb(st, "A2", (128, D), F32)
            B2 = self._sb(st, "B2", (128, D), F32)
            lg = self._sb(st, "lg", (128, D), F32)
            lb = self._sb(st, "lb", (128, D), F32)
            self._bcast_load(lg, ln1g[self.li(l), :])
            self._bcast_load(lb, ln1b[self.li(l), :])
            xt = Rot([self._sb(st, "xt%d" % i, (128, D), F32) for i in range(3)])
            tmpr = Rot([self._sb(st, "tmp%d" % i, (128, D), F32, dma=False) for i in range(2)])
            hb = Rot([self._sb(st, "hb%d" % i, (128, D), BF16, dma=False) for i in range(2)])
            mt = Rot([self._sb(st, "mt%d" % i, (128, KC, 128), BF16) for i in range(2)])
            hst = Rot([self._sb(st, "hst%d" % i, (128, KC, 128), BF16) for i in range(2)])
            sttr = Rot([self._sb(st, "stt%d" % i, (128, 4, 6), F32, dma=False) for i in range(2)])
            mvr = Rot([self._sb(st, "mv%d" % i, (128, 2), F32, dma=False) for i in range(2)])
            rsr = Rot([self._sb(st, "rs%d" % i, (128, 1), F32, dma=False) for i in range(2)])
            tmr = Rot([self._sb(st, "tm%d" % i, (128, 1), F32, dma=False) for i in range(2)])
            pss = [self._ps(st, "ps%d" % i) for i in range(4)]
            ptr = Rot([self._ps(st, "ptr%d" % i, (128, 4, 128), BF16) for i in range(4)])
            state = {"gm": -1, "ab": -1}
            ctxs = {}

            def mm(i):
                m_ = mt.next()
                kb.dma("act", m_[:], mT[:, :, i * 128:(i + 1) * 128].rearrange("c p t -> p c t"), writes=[m_])
                x = xt.next()
                kb.dma("act", x[:], xres[i * 128:(i + 1) * 128, :], writes=[x])
                for n in range(4):
                    kb.mm_group(pss[n], pss[n][:], [m_[:, k, :] for k in range(KC)], [wo[:, k, n * 512:(n + 1) * 512] for k in range(KC)], [m_, won[n]])
                ctxs[i] = {"x": x}

            def ep1(i):
                t_ = tmpr.next()
                ctxs[i]["tmp"] = t_
                self._ln_ep1(pss, t_)

            def ep2(i):
                sset = 0 if i < cfg.NTL else 1
                if sset != state["gm"]:
                    self._bcast_load(gm, mods[l * 2 + sset, 2 * D:3 * D])
                    state["gm"] = sset
                if sset != state["ab"]:
                    self._bcast_load(A2, mods[l * 2 + sset, 4 * D:5 * D])
                    self._bcast_load(B2, mods[l * 2 + sset, 3 * D:4 * D])
                    state["ab"] = sset
                x, t_ = ctxs[i]["x"], ctxs[i]["tmp"]
                self._ln_ep2(gm, x, lg, lb, t_, sttr.next(), mvr.next(), rsr.next(), tmr.next())
                kb.dma("sp", xres[i * 128:(i + 1) * 128, :], x[:], reads=[x])
                h_ = hb.next()
                kb.op("pool", lambda e: e.tensor_tensor(out=t_[:], in0=x[:], in1=A2[:], op=ALU.mult), reads=[x, A2], writes=[t_])
                kb.op("pool", lambda e: e.tensor_tensor(out=h_[:], in0=t_[:], in1=B2[:], op=ALU.add), reads=[t_, B2], writes=[h_])
                hs = hst.next()
                self._transpose_tile(h_, idb, ptr, hs, hs, 0)
                kb.dma("sp", hfT[:, :, i * 128:(i + 1) * 128].rearrange("c p t -> p c t"), hs[:], reads=[hs])
                del ctxs[i]

            n_t = self.tilesB
            mm(0)
            ep1(0)
            for i in range(1, n_t):
                mm(i)
                ep2(i - 1)
                ep1(i)
            ep2(n_t - 1)
            kb.run_stage("outproj")

    def _ffn_up(self, l):
        cfg, kb = self.cfg, self.kb
        T = cfg.T
        w_gu = self.din("w_gu", (self.wdep, D, 2 * FFN))
        hfT = self.dscr("hfT", (KC, 128, T), BF16)
        aT = self.dscr("aT", (cfg.NT, 128, FKC, 128), BF16)
        wl = w_gu[self.li(l)].rearrange("(kc p) n -> p kc n", p=128)
        Tb = self.tilesB * 128
        with ExitStack() as st:
            hf = self._sb(st, "hf", (128, KC, T), BF16)
            for g in range(4):
                kb.dma("sp", hf[:, g * 4:(g + 1) * 4, 0:Tb], hfT[g * 4:(g + 1) * 4, :, 0:Tb].rearrange("c p t -> p c t"), writes=[hf])
            wga = Rot([self._sb(st, "wga%d" % i, (128, KC, 512), BF16) for i in range(2)])
            wua = Rot([self._sb(st, "wua%d" % i, (128, KC, 512), BF16) for i in range(2)])
            stg = Rot([self._sb(st, "stg%d" % i, (128, 8, 512), F32) for i in range(3)])
            pss = Rot([self._ps(st, "ps%d" % i) for i in range(8)])
            sgb = Rot([self._sb(st, "sgb%d" % i, (128, 512), F32, dma=False) for i in range(3)])
            ab = Rot([self._sb(st, "ab%d" % i, (128, 512), BF16) for i in range(3)])
            tasks = []
            for g in range(FKC // 4):
                hold = {}

                def ld(g=g, hold=hold):
                    hold["g"], hold["u"] = wga.next(), wua.next()
                    self._wload2(stg, hold["g"], wl[:, :, g * 512:(g + 1) * 512])
                    self._wload2(stg, hold["u"], wl[:, :, FFN + g * 512:FFN + (g + 1) * 512])

                def cp(g=g, hold=hold):
                    wg_, wu_ = hold["g"], hold["u"]
                    for jj in range(4):
                        j = g * 4 + jj
                        for blk in self.blocksB:
                            t0, bs = blk
                            pg, pu = pss.next(), pss.next()
                            kb.mm_group(pg, pg[:, 0:bs], [wg_[:, k, jj * 128:(jj + 1) * 128] for k in range(KC)], [hf[:, k, t0:t0 + bs] for k in range(KC)], [wg_, hf])
                            kb.mm_group(pu, pu[:, 0:bs], [wu_[:, k, jj * 128:(jj + 1) * 128] for k in range(KC)], [hf[:, k, t0:t0 + bs] for k in range(KC)], [wu_, hf])
                            s_ = sgb.next()
                            kb.op("act", lambda e, s_=s_, pg=pg, bs=bs: e.activation(out=s_[:, 0:bs], in_=pg[:, 0:bs], func=AF.Silu), reads=[pg], writes=[s_])
                            a_ = ab.next()
                            kb.op("dve", lambda e, a_=a_, s_=s_, pu=pu, bs=bs: e.tensor_tensor(out=a_[:, 0:bs], in0=pu[:, 0:bs], in1=s_[:, 0:bs], op=ALU.mult),
                                  reads=[pu, s_], writes=[a_])
                            i0 = t0 // 128
                            nt = bs // 128
                            kb.dma("sp", aT[i0:i0 + nt, :, j, :].rearrange("i p t -> p i t"), a_[:, 0:bs].rearrange("p (i t) -> p i t", t=128), reads=[a_])
                tasks.append((ld, cp))
            self._pipeline(tasks)
            kb.run_stage("ffn_up")

    def _ffn_down(self, l, last):
        cfg, kb = self.cfg, self.kb
        T = cfg.T
        w_down = self.din("w_down", (self.wdep, FFN, D))
        ln2g = self.din("ln2_g", (self.wdep, D))
        ln2b = self.din("ln2_b", (self.wdep, D))
        mods = self._mods()
        xres = self._xres()
        aT = self.dscr("aT", (cfg.NT, 128, FKC, 128), BF16)
        part = self.dscr("part", (T, D), F32)
        outp = self.dout("out", (cfg.LAT, D)) if last else None
        wdl = w_down[self.li(l)].rearrange("(kc p) n -> p kc n", p=128)
        H = FKC // 2
        for ps_i in range(2):
            with ExitStack() as st:
                wd = self._sb(st, "wd", (128, H, D), BF16)
                wdk = [wd] + [kb.buf(wd.t, dma=True, name="wdk%d" % k) for k in range(1, H)]
                stg = Rot([self._sb(st, "stg%d" % i, (128, D), F32) for i in range(2)])
                self._cast_engs = ("act", "dve")
                for k in range(H):
                    self._wload_hw(stg, wd[:, k, :], wdk[k], wdl[:, ps_i * H + k, :])
                self._cast_engs = ("pool", "act")
                at = Rot([self._sb(st, "at%d" % i, (128, H, 128), BF16) for i in range(2)])
                pss = [self._ps(st, "ps%d" % i) for i in range(8)]
                if ps_i == 0:
                    pt = Rot([self._sb(st, "pt%d" % i, (128, D), F32) for i in range(2)])
                else:
                    gf = self._sb(st, "gf", (128, D), F32)
                    lg = self._sb(st, "lg", (128, D), F32)
                    lb = self._sb(st, "lb", (128, D), F32)
                    self._bcast_load(lg, ln2g[self.li(l), :])
                    self._bcast_load(lb, ln2b[self.li(l), :])
                    xt = Rot([self._sb(st, "xt%d" % i, (128, D), F32) for i in range(2)])
                    pt = Rot([self._sb(st, "pt%d" % i, (128, D), F32) for i in range(2)])
                    tmpr = Rot([self._sb(st, "tmp%d" % i, (128, D), F32, dma=False) for i in range(2)])
                    sttr = Rot([self._sb(st, "stt%d" % i, (128, 4, 6), F32, dma=False) for i in range(2)])
                    mvr = Rot([self._sb(st, "mv%d" % i, (128, 2), F32, dma=False) for i in range(2)])
                    rsr = Rot([self._sb(st, "rs%d" % i, (128, 1), F32, dma=False) for i in range(2)])
                    tmr = Rot([self._sb(st, "tm%d" % i, (128, 1), F32, dma=False) for i in range(2)])
                cur = {"set": -1}
                lds = {}

                def loads(i, ps_i=ps_i):
                    a_ = at.next()
                    kb.dma("sp", a_[:], aT[i, :, ps_i * H:(ps_i + 1) * H, :], writes=[a_])
                    d_ = {"a": a_}
                    if ps_i == 1:
                        p_ = pt.next()
                        kb.dma("sp", p_[:], part[i * 128:(i + 1) * 128, :], writes=[p_])
                        x = xt.next()
                        kb.dma("sp", x[:], xres[i * 128:(i + 1) * 128, :], writes=[x])
                        d_["p"], d_["x"] = p_, x
                    lds[i] = d_

                def compute(i, ps_i=ps_i):
                    d_ = lds.pop(i)
                    a_ = d_["a"]
                    p4 = pss[(i % 2) * 4:(i % 2) * 4 + 4]
                    for n in range(4):
                        kb.mm_group(p4[n], p4[n][:], [a_[:, k, :] for k in range(H)], [wd[:, k, n * 512:(n + 1) * 512] for k in range(H)], [a_], reads_k=wdk)
                    if ps_i == 0:
                        p_ = pt.next()
                        for n in range(4):
                            if n % 2 == 0:
                                kb.op("act", lambda e, p_=p_, n=n, p4=p4: e.activation(out=p_[:, n * 512:(n + 1) * 512], in_=p4[n][:], func=AF.Identity),
                                      reads=[p4[n]], writes=[p_])
                            else:
                                kb.op("dve", lambda e, p_=p_, n=n, p4=p4: e.tensor_copy(out=p_[:, n * 512:(n + 1) * 512], in_=p4[n][:]),
                                      reads=[p4[n]], writes=[p_])
                        kb.dma("act", part[i * 128:(i + 1) * 128, :], p_[:], reads=[p_])
                    else:
                        sset = 0 if i < cfg.NTL else 1
                        if sset != cur["set"]:
                            self._bcast_load(gf, mods[l * 2 + sset, 5 * D:6 * D])
                            cur["set"] = sset
                        p_, x = d_["p"], d_["x"]
                        t_ = tmpr.next()
                        self._ln_ep1(p4, t_)
                        self._ln_ep2(gf, x, lg, lb, t_, sttr.next(), mvr.next(), rsr.next(), tmr.next(), extra_add=p_)
                        if last:
                            kb.dma("act", outp[i * 128:(i + 1) * 128, :], x[:], reads=[x])
                        else:
                            kb.dma("act", xres[i * 128:(i + 1) * 128, :], x[:], reads=[x])

                n_t = self.tilesB
                loads(0)
                for i in range(n_t):
                    if i + 1 < n_t:
                        loads(i + 1)
                    compute(i)
                kb.run_stage("ffn_down%d" % ps_i)


def _rope_tables(pos0, lat, grid_w=64):
    p = np.arange(128)
    dd = p % 64
    axis = dd // 32
    a = dd % 32
    half = a // 16
    pair = a % 16
    inv = (10000.0 ** (-np.arange(16, dtype=np.float32) / 16)).astype(np.float32)
    t = pos0 + np.arange(lat)
    row = (t // grid_w).astype(np.float32)
    col = (t % grid_w).astype(np.float32)
    posax = np.where(axis[:, None] == 0, row[None, :], col[None, :]).astype(np.float32)
    ang = (posax * inv[pair][:, None]).astype(np.float32)
    cos = np.cos(ang).astype(np.float32)
    sin = np.sin(ang).astype(np.float32)
    sgn = np.where(half == 0, -1.0, 1.0).astype(np.float32)[:, None]
    c = np.ones((128, lat + 128), np.float32)
    s = np.zeros((128, lat + 128), np.float32)
    c[:, :lat] = cos
    s[:, :lat] = sin * sgn
    return c, s


def _rope_perm():
    p = np.arange(128)
    a = (p % 64) % 32
    half = a // 16
    partner = np.where(half == 0, p + 16, p - 16)
    return partner


def _pcorr(half, lat, nseq_lat, nseq_ctx):
    out = np.ones((2, 4, 16), np.float32)
    for si, (nloc, n) in enumerate(((lat, nseq_lat), (128, nseq_ctx))):
        g0 = half * nloc
        for wi, w in enumerate(WINS):
            for j in range(16):
                tl = j if j < 8 else nloc - 16 + j
                t = g0 + tl
                lo = min(max(t - w // 2, 0), n)
                hi = min(max(t - w // 2 + w, 0), n)
                out[si, wi, j] = w / float(hi - lo)
    return out


_PROG_CACHE = {}


def _get_prog(key, cfg, phases, mods_ext, xchg_ext, fused=False):
    if key not in _PROG_CACHE:
        import time as _t
        t0 = _t.time()
        p = Prog(cfg, phases)
        p.mods_ext = mods_ext
        p.xchg_ext = xchg_ext
        p.fused = fused
        p.wdep = cfg.depth if fused else 1
        p.build()
        if VERBOSE:
            print("build %s: %.1fs" % (str(key), _t.time() - t0), flush=True)
        _PROG_CACHE[key] = p
    return _PROG_CACHE[key]


def _run(prog, per_core):
    in_maps = []
    for c in range(8):
        in_maps.append({n: per_core[c][n] for n in prog.inputs})
    import time as _t
    t0 = _t.time()
    res = run_bass_kernel_spmd(prog.nc, in_maps, core_ids=list(range(8)))
    if VERBOSE:
        nb = sum(v.nbytes for v in in_maps[0].values())
        print("launch: %.1fs, in bytes/core %.1f MB" % (_t.time() - t0, nb / 1e6), flush=True)
    return res.results


def kernel(x, c, ctx, c_ctx, w_ada, b_ada, w_in, lam_qk, subln_g, gmlp_ln_g, gmlp_ln_b,
           w_spatial, b_spatial, w_pool, pool_scale, w_branch, w_out, ln1_g, ln1_b,
           w_gu, w_down, ln2_g, ln2_b):
    f = lambda a: np.ascontiguousarray(np.asarray(a), dtype=np.float32)
    x, c, ctx, c_ctx = f(x), f(c), f(ctx), f(c_ctx)
    B, S, _ = x.shape
    depth = w_in.shape[0]
    lat = S // 2
    cfg = Cfg(lat, depth)
    T = cfg.T
    w_in = f(w_in)
    perm = _rope_perm()
    colperm = np.concatenate([h * 128 + perm for h in range(16)])
    shared = {
        "ident": np.eye(128, dtype=np.float32),
        "w_ada": f(w_ada), "b_ada": f(b_ada), "w_in": w_in,
        "w_qkp": np.ascontiguousarray(w_in[:, :, :2048][:, :, colperm]),
        "lam_qk": f(lam_qk).reshape(depth, 256),
        "subln_g": f(subln_g).reshape(depth, 128, 1),
        "gmlp_ln_g": f(gmlp_ln_g), "gmlp_ln_b": f(gmlp_ln_b),
        "wsT": np.ascontiguousarray(f(w_spatial).transpose(0, 3, 1, 2)),
        "b_spatial": f(b_spatial).reshape(depth, 1024),
        "w_pool": f(w_pool),
        "pool_scaleT": np.ascontiguousarray(f(pool_scale).reshape(depth, 8, 128).transpose(0, 2, 1)),
        "w_branch": f(w_branch), "w_out": f(w_out), "ln1_g": f(ln1_g), "ln1_b": f(ln1_b),
        "w_gu": f(w_gu), "w_down": f(w_down), "ln2_g": f(ln2_g), "ln2_b": f(ln2_b),
    }
    per_core = []
    for core in range(8):
        b, half = core // 2, core % 2
        d = dict(shared)
        d["x_raw"] = np.concatenate([x[b, half * lat:(half + 1) * lat], ctx[b, half * 128:(half + 1) * 128]], axis=0)
        cc = np.stack([c[b], c_ctx], axis=1)
        d["cT"] = np.ascontiguousarray(cc.reshape(KC, 128, 2).transpose(1, 0, 2))
        rc, rs_ = _rope_tables(half * lat, lat)
        d["rope_cos"], d["rope_sin"] = rc, rs_
        hm = np.zeros((128, 2), np.float32)
        hm[:, 0] = 1.0 if half == 1 else 0.0
        hm[:, 1] = 1.0 if half == 0 else 0.0
        d["hmask"] = hm
        d["pcorr"] = np.ascontiguousarray(np.broadcast_to(_pcorr(half, lat, S, 256)[None], (128, 2, 4, 16)))
        per_core.append(d)

    LAYER_W = ["w_in", "w_qkp", "lam_qk", "subln_g", "wsT", "b_spatial", "gmlp_ln_g", "gmlp_ln_b", "w_pool",
               "pool_scaleT", "w_branch", "w_out", "ln1_g", "ln1_b", "w_gu", "w_down", "ln2_g", "ln2_b"]

    def layer_inputs(l):
        for core in range(8):
            for n in LAYER_W:
                per_core[core][n] = shared[n][l:l + 1]

    if FUSED:
        phases = [("norm0", 0), ("mods", 0)]
        for l in range(depth):
            phases += [("A", l), ("xchg", l), ("B", l)]
        pf = _get_prog(("F", lat, depth), cfg, phases, False, False, fused=True)
        rf = _run(pf, per_core)
        out = np.zeros((B, S, D), np.float32)
        for core in range(8):
            b, half = core // 2, core % 2
            out[b, half * lat:(half + 1) * lat] = rf[core]["out"]
        return out

    p0 = _get_prog(("L0", lat, depth), cfg, [("norm0", 0), ("mods", 0), ("xout", 0)], True, True)
    r0 = _run(p0, per_core)
    for core in range(8):
        per_core[core]["x_in"] = r0[core]["x_out"]
        per_core[core]["mods"] = r0[core]["mods"]
    out = None
    for l in range(depth):
        last = l == depth - 1
        layer_inputs(l)
        pa = _get_prog(("A", lat, depth, l), cfg, [("xin", l), ("A", l)], True, True)
        ra = _run(pa, per_core)
        for core in range(8):
            pr = [2 * (core // 2), 2 * (core // 2) + 1]
            for h in range(8):
                per_core[core]["KT_all%d" % h] = np.concatenate([ra[q]["KT_x%d" % h] for q in pr], axis=0)
                per_core[core]["V_all%d" % h] = np.concatenate([ra[q]["V_x%d" % h] for q in pr], axis=0)
            per_core[core]["halo_all"] = np.concatenate([ra[q]["halo_x"] for q in pr], axis=0)
        phases = [("xin", l), ("B", l)] + ([] if last else [("xout", l)])
        pb = _get_prog(("B", lat, depth, l), cfg, phases, True, True)
        rb = _run(pb, per_core)
        if DEBUG is not None:
            DEBUG.append((ra, rb))
        if last:
            out = np.zeros((B, S, D), np.float32)
            for core in range(8):
                b, half = core // 2, core % 2
                out[b, half * lat:(half + 1) * lat] = rb[core]["out"]
        else:
            for core in range(8):
                per_core[core]["x_in"] = rb[core]["x_out"]
    return out


DEBUG = None
FUSED = True
VERBOSE = False
```
